# Optimizing a Trainium2 kernel written in Bass

```python
import jax, jax.numpy as jnp
from jax import lax
import numpy as np

D_MODEL = 1024
BATCH = 32
SEQ = 2048
DEPTH = 2
DEC_BATCH = 32
DEC_SEQ = 64
PAST_LEN = 1024

CHUNK = 64
Q_BLOCK = 128
EPS = 1e-6

SSD_HEADS = 8
SSD_HEAD_DIM = 64
SSD_WIDTH = SSD_HEADS * SSD_HEAD_DIM
SSD_GROUPS = 2
D_STATE = 128
CONV_W = 4
CONV_DIM = SSD_WIDTH + 2 * SSD_GROUPS * D_STATE

MLA_HEADS = 8
QK_NOPE = 64
QK_ROPE = 32
V_HEAD = 64
Q_LORA = 256
KV_LORA = 128
MLA_WIDTH = MLA_HEADS * V_HEAD
ROPE_BASE = 10000.0

MIX_WIDTH = SSD_WIDTH + MLA_WIDTH
S_Z = SSD_WIDTH
S_XBC = S_Z + CONV_DIM
S_DT = S_XBC + SSD_HEADS
S_Q = S_DT + Q_LORA
IN_PROJ = S_Q + KV_LORA + QK_ROPE

D_FF = -(-(8 * D_MODEL) // (3 * 256)) * 256

kernel_name = "hybrid_ssd_mla_streaming_step"


def rmsnorm(x, w):
    xf = x.astype(jnp.float32)
    y = xf * lax.rsqrt(jnp.mean(xf * xf, axis=-1, keepdims=True) + EPS)
    return (y * w.astype(jnp.float32)).astype(x.dtype)


def rope_tables(pos):
    inv = 1.0 / (ROPE_BASE ** (jnp.arange(0, QK_ROPE, 2, dtype=jnp.float32) / QK_ROPE))
    ang = pos.astype(jnp.float32)[:, None] * inv[None, :]
    return jnp.cos(ang), jnp.sin(ang)


def apply_rope(x, cos, sin):
    x1, x2 = jnp.split(x.astype(jnp.float32), 2, axis=-1)
    return jnp.concatenate([x1 * cos - x2 * sin, x1 * sin + x2 * cos], axis=-1).astype(x.dtype)


def causal_conv(u, buf, w, b):
    T = u.shape[1]
    up = jnp.concatenate([buf.astype(u.dtype), u], axis=1)
    y = b
    for k in range(CONV_W):
        y = y + up[:, k:k + T] * w[k]
    return jax.nn.silu(y), up[:, -(CONV_W - 1):]


def ssd_scan(xh, dt, A, Bm, Cm, h0):
    b, T = xh.shape[:2]
    L = min(CHUNK, T)
    nc = T // L
    hg = SSD_HEADS // SSD_GROUPS
    f32 = jnp.float32

    def to_chunks(a):
        return jnp.moveaxis(a.reshape((b, nc, L) + a.shape[2:]), 1, 0)

    xc = to_chunks(xh.astype(f32).reshape(b, T, SSD_GROUPS, hg, SSD_HEAD_DIM))
    dtc = to_chunks(dt.reshape(b, T, SSD_GROUPS, hg))
    Bc = to_chunks(Bm.astype(f32))
    Cc = to_chunks(Cm.astype(f32))
    causal = jnp.tril(jnp.ones((L, L), dtype=bool))[None, :, :, None, None]
    A_g = A.reshape(SSD_GROUPS, hg)

    def step(h, inp):
        x, d, Bk, Ck = inp
        cum = jnp.cumsum(d * A_g, axis=1)
        seg = cum[:, :, None] - cum[:, None, :]
        decay = jnp.where(causal, jnp.exp(jnp.where(causal, seg, 0.0)), 0.0)
        cb = jnp.einsum('blgn,bsgn->blsg', Ck, Bk)
        wts = decay * cb[..., None] * d[:, None]
        y = jnp.einsum('blsgh,bsghp->blghp', wts, x)
        y = y + jnp.einsum('blgn,bghpn,blgh->blghp', Ck, h, jnp.exp(cum))
        last = cum[:, -1]
        w_end = jnp.exp(last[:, None] - cum) * d
        h_new = jnp.exp(last)[..., None, None] * h + jnp.einsum('blgh,blghp,blgn->bghpn', w_end, x, Bk)
        return h_new, y

    h_init = h0.astype(f32).reshape(b, SSD_GROUPS, hg, SSD_HEAD_DIM, D_STATE)
    hT, ys = lax.scan(step, h_init, (xc, dtc, Bc, Cc))
    y = jnp.moveaxis(ys, 0, 1).reshape(b, T, SSD_HEADS, SSD_HEAD_DIM)
    return y, hT.reshape(b, SSD_HEADS, SSD_HEAD_DIM, D_STATE).astype(h0.dtype)


def mla_attend(q_lat, q_rope, q_pos, keys_c, keys_r, k_pos):
    b, T = q_lat.shape[:2]
    qb = min(Q_BLOCK, T)
    nb = T // qb
    scale = (QK_NOPE + QK_ROPE) ** -0.5
    k_chunk = k_pos // CHUNK

    def block(args):
        ql, qr, qp = args
        s = (jnp.einsum('bqhl,bkl->bhqk', ql, keys_c, preferred_element_type=jnp.float32)
             + jnp.einsum('bqhr,bkr->bhqk', qr, keys_r, preferred_element_type=jnp.float32))
        mask = k_chunk[None, :] <= (qp // CHUNK)[:, None]
        s = jnp.where(mask, s * scale, -jnp.inf)
        pr = jax.nn.softmax(s, axis=-1).astype(keys_c.dtype)
        return jnp.einsum('bhqk,bkl->bqhl', pr, keys_c)

    def split(a):
        return jnp.moveaxis(a.reshape((b, nb, qb) + a.shape[2:]), 1, 0)

    out = lax.map(block, (split(q_lat), split(q_rope), q_pos.reshape(nb, qb)))
    return jnp.moveaxis(out, 0, 1).reshape(b, T, MLA_HEADS, KV_LORA)


def hybrid_layer(x, ckv_past, krope_past, ssm_h0, conv_buf, p):
    b, T, _ = x.shape
    P = ckv_past.shape[1]
    h = rmsnorm(x, p['norm_mix'])
    proj = h @ p['w_in']
    z = proj[..., :S_Z]
    xbc = proj[..., S_Z:S_XBC]
    dt_raw = proj[..., S_XBC:S_DT]
    cq = proj[..., S_DT:S_Q]
    ckv_raw = proj[..., S_Q:]

    xbc, conv_new = causal_conv(xbc, conv_buf, p['conv_w'], p['conv_b'])
    xs = xbc[..., :SSD_WIDTH]
    Bm = xbc[..., SSD_WIDTH:SSD_WIDTH + SSD_GROUPS * D_STATE].reshape(b, T, SSD_GROUPS, D_STATE)
    Cm = xbc[..., SSD_WIDTH + SSD_GROUPS * D_STATE:].reshape(b, T, SSD_GROUPS, D_STATE)
    dt = jax.nn.softplus(dt_raw.astype(jnp.float32) + p['dt_bias'].astype(jnp.float32))
    A = -jnp.exp(p['a_log'].astype(jnp.float32))
    xh = xs.reshape(b, T, SSD_HEADS, SSD_HEAD_DIM)
    y, h_new = ssd_scan(xh, dt, A, Bm, Cm, ssm_h0)
    y = y + p['d_skip'].astype(jnp.float32)[:, None] * xh.astype(jnp.float32)
    y = y.reshape(b, T, SSD_WIDTH) * jax.nn.silu(z.astype(jnp.float32))
    y_ssd = rmsnorm(y, p['ssd_norm']).astype(x.dtype)

    q_pos = P + jnp.arange(T)
    cos, sin = rope_tables(q_pos)
    q = (rmsnorm(cq, p['q_norm']) @ p['w_uq']).reshape(b, T, MLA_HEADS, QK_NOPE + QK_ROPE)
    q_nope = q[..., :QK_NOPE]
    q_rope = apply_rope(q[..., QK_NOPE:], cos[:, None], sin[:, None])
    c_kv = rmsnorm(ckv_raw[..., :KV_LORA], p['kv_norm'])
    k_rope = apply_rope(ckv_raw[..., KV_LORA:], cos, sin)
    q_lat = jnp.einsum('bthd,lhd->bthl', q_nope, p['w_uk'])
    keys_c = jnp.concatenate([ckv_past.astype(c_kv.dtype), c_kv], axis=1)
    keys_r = jnp.concatenate([krope_past.astype(k_rope.dtype), k_rope], axis=1)
    o_lat = mla_attend(q_lat, q_rope, q_pos, keys_c, keys_r, jnp.arange(P + T))
    o = jnp.einsum('bthl,lhd->bthd', o_lat, p['w_uv']).reshape(b, T, MLA_WIDTH)

    x = x + jnp.concatenate([y_ssd, o], axis=-1) @ p['w_out']
    hf = rmsnorm(x, p['norm_ffn'])
    x = x + (jax.nn.silu(hf @ p['w_gate']) * (hf @ p['w_up'])) @ p['w_down']
    return x, c_kv, k_rope, h_new, conv_new


def setup_inputs(seed: int = 0) -> dict:
    key = jax.random.key(seed)
    ks = jax.random.split(key, 32)
    f32 = jnp.float32
    nrm = lambda k, shape, s: jax.random.normal(k, shape, f32) * s
    gain = lambda k, shape: 1.0 + 0.02 * jax.random.normal(k, shape, f32)
    dt0 = jnp.exp(jax.random.uniform(ks[20], (DEPTH, SSD_HEADS), f32, np.log(1e-3), np.log(1e-1)))
    return {
        "x_prompt": jax.random.normal(ks[0], (BATCH, SEQ, D_MODEL), f32),
        "x_sample": jax.random.normal(ks[1], (DEC_BATCH, DEC_SEQ, D_MODEL), f32),
        "cache_mla_ckv": jax.random.normal(ks[2], (DEPTH, DEC_BATCH, PAST_LEN, KV_LORA), f32),
        "cache_mla_krope": jax.random.normal(ks[3], (DEPTH, DEC_BATCH, PAST_LEN, QK_ROPE), f32),
        "state_ssm": nrm(ks[4], (DEPTH, DEC_BATCH, SSD_HEADS, SSD_HEAD_DIM, D_STATE), 0.1),
        "state_conv": jax.random.normal(ks[5], (DEPTH, DEC_BATCH, CONV_W - 1, CONV_DIM), f32),
        "w_in": nrm(ks[6], (DEPTH, D_MODEL, IN_PROJ), D_MODEL ** -0.5),
        "w_uq": nrm(ks[7], (DEPTH, Q_LORA, MLA_HEADS * (QK_NOPE + QK_ROPE)), Q_LORA ** -0.5),
        "w_uk": nrm(ks[8], (DEPTH, KV_LORA, MLA_HEADS, QK_NOPE), KV_LORA ** -0.5),
        "w_uv": nrm(ks[9], (DEPTH, KV_LORA, MLA_HEADS, V_HEAD), KV_LORA ** -0.5),
        "w_out": nrm(ks[10], (DEPTH, MIX_WIDTH, D_MODEL), MIX_WIDTH ** -0.5),
        "norm_mix": gain(ks[11], (DEPTH, D_MODEL)),
        "q_norm": gain(ks[12], (DEPTH, Q_LORA)),
        "kv_norm": gain(ks[13], (DEPTH, KV_LORA)),
        "ssd_norm": gain(ks[14], (DEPTH, SSD_WIDTH)),
        "conv_w": nrm(ks[15], (DEPTH, CONV_W, CONV_DIM), CONV_W ** -0.5),
        "conv_b": nrm(ks[16], (DEPTH, CONV_DIM), 0.01),
        "dt_bias": dt0 + jnp.log(-jnp.expm1(-dt0)),
        "a_log": jnp.log(jax.random.uniform(ks[17], (DEPTH, SSD_HEADS), f32, 1.0, 16.0)),
        "d_skip": 1.0 + 0.1 * jax.random.normal(ks[18], (DEPTH, SSD_HEADS), f32),
        "norm_ffn": gain(ks[19], (DEPTH, D_MODEL)),
        "w_gate": nrm(ks[21], (DEPTH, D_MODEL, D_FF), D_MODEL ** -0.5),
        "w_up": nrm(ks[22], (DEPTH, D_MODEL, D_FF), D_MODEL ** -0.5),
        "w_down": nrm(ks[23], (DEPTH, D_FF, D_MODEL), D_FF ** -0.5),
        "norm_final": gain(ks[24], (D_MODEL,)),
    }


def reference(x_prompt, x_sample, cache_mla_ckv, cache_mla_krope, state_ssm, state_conv,
              w_in, w_uq, w_uk, w_uv, w_out, norm_mix, q_norm, kv_norm, ssd_norm,
              conv_w, conv_b, dt_bias, a_log, d_skip, norm_ffn, w_gate, w_up, w_down, norm_final):
    xp, xs = x_prompt, x_sample
    bp = xp.shape[0]
    p_ckv, p_kr, p_ssm, p_conv = [], [], [], []
    s_ckv, s_kr, s_ssm, s_conv = [], [], [], []
    for l in range(DEPTH):
        p = {"w_in": w_in[l], "w_uq": w_uq[l], "w_uk": w_uk[l], "w_uv": w_uv[l], "w_out": w_out[l],
             "norm_mix": norm_mix[l], "q_norm": q_norm[l], "kv_norm": kv_norm[l], "ssd_norm": ssd_norm[l],
             "conv_w": conv_w[l], "conv_b": conv_b[l], "dt_bias": dt_bias[l], "a_log": a_log[l],
             "d_skip": d_skip[l], "norm_ffn": norm_ffn[l], "w_gate": w_gate[l], "w_up": w_up[l],
             "w_down": w_down[l]}
        xp, ckv, kr, hs, cb = hybrid_layer(
            xp,
            jnp.zeros((bp, 0, KV_LORA), xp.dtype),
            jnp.zeros((bp, 0, QK_ROPE), xp.dtype),
            jnp.zeros((bp, SSD_HEADS, SSD_HEAD_DIM, D_STATE), state_ssm.dtype),
            jnp.zeros((bp, CONV_W - 1, CONV_DIM), xp.dtype),
            p)
        p_ckv.append(ckv); p_kr.append(kr); p_ssm.append(hs); p_conv.append(cb)
        xs, ckv, kr, hs, cb = hybrid_layer(xs, cache_mla_ckv[l], cache_mla_krope[l], state_ssm[l], state_conv[l], p)
        s_ckv.append(ckv); s_kr.append(kr); s_ssm.append(hs); s_conv.append(cb)
    y_prompt = rmsnorm(xp, norm_final)
    y_sample = rmsnorm(xs, norm_final)
    return (y_prompt, y_sample,
            jnp.stack(p_ckv), jnp.stack(p_kr), jnp.stack(p_ssm), jnp.stack(p_conv),
            jnp.stack(s_ckv), jnp.stack(s_kr), jnp.stack(s_ssm), jnp.stack(s_conv))
```

```python
import numpy as np
import concourse.bass as bass
import concourse.mybir as mybir
from concourse.bass_utils import run_bass_kernel_spmd

F32 = mybir.dt.float32
BF16 = mybir.dt.bfloat16
AF = mybir.ActivationFunctionType
ALU = mybir.AluOpType

D_MODEL = 1024
EPS = 1e-6
SSD_HEADS = 8
D_STATE = 128
CONV_W = 4
MLA_HEADS = 8
QK_NOPE = 64
QK_ROPE = 32
Q_LORA = 256
KV_LORA = 128
S_Z = 512
S_XBC = 1536
S_DT = 1544
S_Q = 1800
IN_PROJ = 1960
D_FF = 2816
NFB = D_FF // 128
ROPE_BASE = 10000.0
SCALE = (QK_NOPE + QK_ROPE) ** -0.5

C_Z, C_DTKV, C_X0, C_X1, C_CQ, C_UQ, C_O0, C_O1 = range(8)
C_GU = 8
C_D = 19
NCHUNK = 25


class Cfg:
    def __init__(self, PB=4, SEQ=2048, SB=4, PAST=1024, DEPTH=2):
        self.PB, self.SEQ, self.SB, self.PAST, self.DEPTH = PB, SEQ, SB, PAST, DEPTH
        self.DS = 64
        self.NTP = SEQ // 512


class Tk:
    __slots__ = ("w", "r", "name", "busy", "hold", "stamp")

    def __init__(self, name=""):
        self.w = None
        self.r = {}
        self.name = name
        self.busy = False
        self.hold = False
        self.stamp = 0


class Eng:
    def __init__(self, S, name, e, is_pe=False):
        self.S = S
        self.name = name
        self.e = e
        self.is_pe = is_pe
        self.sem = S.nc.alloc_semaphore("sem_" + name)
        self.key = ("E", name)
        S.sems[self.key] = self.sem
        self.cnt = 0
        self.seen = {}
        self.nwait = 0
        self.nins = 0


class Sched:
    NDMA = 8

    def __init__(self, nc):
        self.nc = nc
        self.sems = {}
        self.stampc = 0
        self.pe = Eng(self, "pe", nc.tensor, True)
        self.act = Eng(self, "act", nc.scalar)
        self.dve = Eng(self, "dve", nc.vector)
        self.pool = Eng(self, "pool", nc.gpsimd)
        self.sp = Eng(self, "sp", nc.sync)
        self.dq = {}
        for q in (self.sp, self.pool):
            sl = []
            for i in range(self.NDMA):
                key = ("D", q.name, i)
                self.sems[key] = nc.alloc_semaphore("dsem_%s_%d" % (q.name, i))
                sl.append(key)
            self.dq[q.name] = [sl, 0]

    def _wait(self, eng, deps):
        for key, val in deps:
            if eng.is_pe and key == eng.key:
                continue
            if eng.seen.get(key, 0) >= val:
                continue
            eng.e.wait_ge(self.sems[key], val)
            eng.seen[key] = val
            eng.nwait += 1

    def _deps(self, reads, writes):
        deps = {}
        for t in reads:
            if t.w is not None:
                k, v = t.w
                if deps.get(k, 0) < v:
                    deps[k] = v
        for t in writes:
            if t.w is not None:
                k, v = t.w
                if deps.get(k, 0) < v:
                    deps[k] = v
            for k, v in t.r.items():
                if deps.get(k, 0) < v:
                    deps[k] = v
        return list(deps.items())

    def _mark(self, ticket, reads, writes):
        k, v = ticket
        for t in reads:
            if t.r.get(k, 0) < v:
                t.r[k] = v
            if t.busy and not t.hold:
                t.busy = False
                self.stampc += 1
                t.stamp = self.stampc
        for t in writes:
            t.w = ticket
            t.r = {}

    def op(self, eng, fn, reads=(), writes=(), sig=True):
        self._wait(eng, self._deps(reads, writes))
        ins = fn()
        eng.nins += 1
        if sig:
            eng.cnt += 1
            ins.then_inc(eng.sem, 1)
            ticket = (eng.key, eng.cnt)
        else:
            assert eng.is_pe
            ticket = (eng.key, eng.cnt + 1)
        self._mark(ticket, reads, writes)
        return ins

    def dma(self, q, out, in_, reads=(), writes=(), **kw):
        self._wait(q, self._deps(reads, writes))
        sl, k = self.dq[q.name]
        key = sl[k % self.NDMA]
        rnd = k // self.NDMA
        if rnd > 0:
            self._wait(q, [(key, 16 * rnd)])
        ins = q.e.dma_start(out=out, in_=in_, **kw)
        ins.then_inc(self.sems[key], 16)
        ticket = (key, 16 * (rnd + 1))
        self.dq[q.name][1] = k + 1
        q.nins += 1
        self._mark(ticket, reads, writes)
        return ticket

    def finish(self):
        for qn, (sl, k) in self.dq.items():
            q = {"sp": self.sp, "pool": self.pool}[qn]
            for i in range(min(k, self.NDMA)):
                last = k - 1 - i
                key = sl[last % self.NDMA]
                self._wait(q, [(key, 16 * (last // self.NDMA + 1))])


class TileD:
    def __init__(self, cfg, kind, b=0, t=0):
        self.kind = kind
        self.b = b
        self.t = t
        if kind == "p":
            self.S, self.PS = 4, 128
            self.NSEG, self.LSEG = 1, 512
        else:
            self.S, self.PS = cfg.SB, 64
            self.NSEG, self.LSEG = cfg.SB, 64
        self.NT = self.S * self.PS


def build(cfg):
    nc = bass.Bass("TRN2", target_bir_lowering=False)
    S = Sched(nc)
    P, A, V, G = nc.tensor, nc.scalar, nc.vector, nc.gpsimd
    PB, SEQ, SB, PAST, DEPTH, DS, NTP = cfg.PB, cfg.SEQ, cfg.SB, cfg.PAST, cfg.DEPTH, cfg.DS, cfg.NTP
    NPK = PAST // 128
    NKT = max(SEQ // 128, NPK + 1)
    KCAP = NKT * 128

    def din(name, shape, dt=F32):
        return nc.dram_tensor(name, list(shape), dt, kind="ExternalInput").ap()

    def dout(name, shape, dt=F32):
        return nc.dram_tensor(name, list(shape), dt, kind="ExternalOutput").ap()

    x_prompt = din("x_prompt", [PB, SEQ, D_MODEL])
    x_sample = din("x_sample", [SB, DS, D_MODEL])
    cache_ckv = din("cache_mla_ckv", [DEPTH, SB, PAST, KV_LORA])
    cache_kr = din("cache_mla_krope", [DEPTH, SB, PAST, QK_ROPE])
    state_ssm = din("state_ssm", [DEPTH, SB, SSD_HEADS, 64, D_STATE])
    state_conv = din("state_conv", [DEPTH, SB, 3, 1024])
    w_in = din("w_in", [DEPTH, D_MODEL, IN_PROJ])
    w_uq = din("w_uq", [DEPTH, Q_LORA, 768])
    w_uk = din("w_uk", [DEPTH, KV_LORA, 8, 64])
    w_uv = din("w_uv", [DEPTH, KV_LORA, 8, 64])
    w_out = din("w_out", [DEPTH, 1024, D_MODEL])
    norm_mix = din("norm_mix", [DEPTH, D_MODEL])
    q_norm = din("q_norm", [DEPTH, Q_LORA])
    kv_norm = din("kv_norm", [DEPTH, KV_LORA])
    ssd_norm = din("ssd_norm", [DEPTH, 512])
    conv_w = din("conv_w", [DEPTH, CONV_W, 1024])
    conv_b = din("conv_b", [DEPTH, 1024])
    dt_bias = din("dt_bias", [DEPTH, 8])
    a_log = din("a_log", [DEPTH, 8])
    d_skip = din("d_skip", [DEPTH, 8])
    norm_ffn = din("norm_ffn", [DEPTH, D_MODEL])
    w_gate = din("w_gate", [DEPTH, D_MODEL, D_FF])
    w_up = din("w_up", [DEPTH, D_MODEL, D_FF])
    w_down = din("w_down", [DEPTH, D_FF, D_MODEL])
    norm_final = din("norm_final", [D_MODEL])
    consts_d = din("consts", [128, 5 * 128])
    rope_fm_d = din("rope_fm", [128, 2, SEQ + DS])
    rope_tm_d = din("rope_tm", [SEQ + DS, 32])

    y_prompt = dout("y_prompt", [PB, SEQ, D_MODEL])
    y_sample = dout("y_sample", [SB, DS, D_MODEL])
    o_ckv_p = dout("o_ckv_p", [DEPTH, PB, SEQ, KV_LORA])
    o_kr_p = dout("o_kr_p", [DEPTH, PB, SEQ, QK_ROPE])
    o_ssm_p = dout("o_ssm_p", [DEPTH, PB, 8, 64, D_STATE])
    o_conv_p = dout("o_conv_p", [DEPTH, PB, 3, 1024])
    o_ckv_s = dout("o_ckv_s", [DEPTH, SB, DS, KV_LORA])
    o_kr_s = dout("o_kr_s", [DEPTH, SB, DS, QK_ROPE])
    o_ssm_s = dout("o_ssm_s", [DEPTH, SB, 8, 64, D_STATE])
    o_conv_s = dout("o_conv_s", [DEPTH, SB, 3, 1024])

    scr = nc.dram_tensor("wscratch", [DEPTH, NCHUNK, 128, 4096], BF16, kind="Internal").ap()
    T_scr = [[Tk("scr%d_%d" % (l, c)) for c in range(NCHUNK)] for l in range(DEPTH)]

    def sb(name, shape, dt=F32):
        return nc.alloc_sbuf_tensor(name, list(shape), dt).ap()

    cst = sb("cst", [128, 5 * 128]); T_cst = Tk("cst")
    identf = cst[:, 0:128]
    Uf = cst[:, 128:256]
    Mstf = cst[:, 256:384]
    onesf = cst[:, 384:512]
    hmask = cst[:, 512:516]
    cstb = sb("cstb", [128, 3 * 128], BF16); T_cstb = Tk("cstb")
    identb = cstb[:, 0:128]
    onesb = cstb[:, 128:256]
    epsb = sb("epsb", [128, 2]); T_eps = Tk("eps")

    NSLOT = 4
    wring = [sb("wring%d" % i, [128, 4096], BF16) for i in range(NSLOT)]
    T_wring = [Tk("wring%d" % i) for i in range(NSLOT)]

    xt = sb("xt", [128, 4, 1024]); T_xts = [Tk("xt%d" % i) for i in range(4)]
    nst = [sb("nst%d" % i, [128, 4]) for i in range(4)]; T_nst = [Tk("nst%d" % i) for i in range(4)]
    PHB = sb("PHB", [128, 4096], BF16)
    xn = PHB.rearrange("p (s d) -> p s d", s=4)
    cq2 = PHB[:, 0:1024].rearrange("p (k n) -> p k n", k=2)
    cqn = PHB[:, 1024:2048].rearrange("p (k n) -> p k n", k=2)
    rstdb = PHB[:, 2048:3072].bitcast(F32)
    sqtmp = PHB[:, 3072:4096].bitcast(F32)
    T_PHB = Tk("PHB")
    hT = sb("hT", [128, 8, 512], BF16); T_hT = Tk("hT")
    PHA = sb("PHA", [128, NFB * 512], BF16)
    hmid = PHA.rearrange("p (f n) -> p f n", f=NFB)
    convout = PHA[:, 0:4096].rearrange("p (k n) -> p k n", k=8)
    mixtok = PHA[:, 4096:6144].rearrange("p (s d) -> p s d", s=4)
    mixT = PHA[:, 6144:10240].rearrange("p (k n) -> p k n", k=8)
    T_hmid = Tk("hmid"); T_convout = Tk("convout"); T_mixtok = Tk("mixtok"); T_mixT = Tk("mixT")
    junk = sb("junk", [128, 1024], BF16); T_junk = Tk("junk")
    nstat = sb("nstat", [128, 16]); T_nstat = Tk("nstat")
    NCB = 2
    utmp = [sb("utmp%d" % i, [128, 516]) for i in range(NCB)]
    T_utmp = [Tk("utmp%d" % i) for i in range(NCB)]
    cacc = [sb("cacc%d" % i, [128, 512]) for i in range(NCB)]
    T_cacc = [Tk("cacc%d" % i) for i in range(NCB)]
    zs = sb("zs", [128, 4, 512], BF16); T_zs = Tk("zs")
    tk = sb("tk", [128, 4, 168]); T_tk = Tk("tk")
    ckvn = sb("ckvn", [128, 4, 128]); T_ckvn = Tk("ckvn")
    krn = sb("krn", [128, 4, 32]); T_krn = Tk("krn")
    ckvb = sb("ckvb", [128, 4, 128], BF16); T_ckvb = Tk("ckvb")
    kr4b = sb("kr4b", [128, 4, 128], BF16); T_kr4b = Tk("kr4b")
    dtt = sb("dtt", [128, 4, 8]); T_dtt = Tk("dtt")
    att = sb("att", [128, 4, 8]); T_att = Tk("att")
    dtmp = sb("dtmp", [128, 6, 64]); T_dtmp = Tk("dtmp")
    ropetm = sb("ropetm", [128, 4, 32]); T_ropetm = Tk("ropetm")
    ropefm = sb("ropefm", [128, 2, 512]); T_ropefm = Tk("ropefm")
    XBt = [sb("XBt%d" % i, [128, 768], BF16) for i in range(2)]
    T_XBt = [Tk("XBt0"), Tk("XBt1")]
    Rt = sb("Rt", [128, 8, 128]); T_Rt = Tk("Rt")
    Et = sb("Et", [128, 8, 128]); T_Et = Tk("Et")
    CBm = sb("CBm", [128, 2, 128]); T_CBm = Tk("CBm")
    Wb = sb("Wb", [128, 8, 128], BF16); T_Wb = Tk("Wb")
    xD = sb("xD", [128, 512], BF16); T_xD = Tk("xD")
    xw = sb("xw", [128, 512], BF16); T_xw = Tk("xw")
    ex2 = [sb("ex%d" % i, [128, 32]) for i in range(2)]; T_ex2 = [Tk("ex0"), Tk("ex1")]
    ytmp = sb("ytmp", [128, 512]); T_ytmp = Tk("ytmp")
    yg = sb("yg", [128, 512]); T_yg = Tk("yg")
    hTf = [sb("hTf%d" % l, [128, 512]) for l in range(DEPTH)]
    hTb = [sb("hTb%d" % l, [128, 512], BF16) for l in range(DEPTH)]
    T_hT_st = [Tk("hst%d" % l) for l in range(DEPTH)]
    sttmp = sb("sttmp", [128, 4, 128]); T_sttmp = Tk("sttmp")
    convhist = [sb("convhist%d" % l, [128, 8, 4, 3]) for l in range(DEPTH)]
    T_convhist = [Tk("ch%d" % l) for l in range(DEPTH)]
    convhist_s = [sb("convhist_s%d" % l, [128, 8, 4, 3]) for l in range(DEPTH)]
    T_convhist_s = [Tk("chs%d" % l) for l in range(DEPTH)]
    ckvT = [sb("ckvT%d" % l, [128, KCAP], BF16) for l in range(DEPTH)]
    kr4T = [sb("kr4T%d" % l, [128, KCAP], BF16) for l in range(DEPTH)]
    ckvtok = [sb("ckvtok%d" % l, [128, NKT, 128], BF16) for l in range(DEPTH)]
    T_cache = [Tk("cache%d" % l) for l in range(DEPTH)]
    qn_h = [sb("qn%d" % i, [64, 512], BF16) for i in range(2)]
    T_qn = [Tk("qn0"), Tk("qn1")]
    qlat_h = [sb("qlat%d" % i, [128, 512], BF16) for i in range(2)]
    T_qlat = [Tk("qlat0"), Tk("qlat1")]
    qr4 = [sb("qr4_%d" % i, [128, 512], BF16) for i in range(2)]
    T_qr4 = [Tk("qr40"), Tk("qr41")]
    qrz = [sb("qrz%d" % i, [128, 512], BF16) for i in range(2)]
    T_qrz = [Tk("qrz0"), Tk("qrz1")]
    ropet = [ytmp, yg]
    T_ropet = [T_ytmp, T_yg]
    NPT = 3
    PTf = [sb("PTf%d" % i, [128, 512], BF16) for i in range(NPT)]
    T_PTf = [Tk("PTf%d" % i) for i in range(NPT)]
    PTd = [sb("PTd%d" % i, [128, 512], BF16) for i in range(4)]
    T_PTd = [Tk("PTd%d" % i) for i in range(4)]
    rz = cacc[0]; T_rz = T_cacc[0]
    olat2 = [sb("olat%d" % i, [128, 512], BF16) for i in range(2)]; T_olat2 = [Tk("olat0"), Tk("olat1")]
    sg = [XBt[0][:, 0:512], XBt[1][:, 0:512]]
    T_sg = [T_XBt[0], T_XBt[1]]

    pastkr = sb("pastkr", [128, NPK, 32]); T_pastkr = Tk("pastkr")
    pastb = sb("pastb", [128, NPK, 128], BF16); T_pastb = Tk("pastb")
    pastkb = sb("pastkb", [128, NPK, 128], BF16); T_pastkb = Tk("pastkb")
    NPRM = DEPTH * (8 + 8 + 4 + 2 + 32 + 8 + 8 + 8 + 8 + 8) + 8
    prm = sb("prm", [128, DEPTH, 96]); T_prm = Tk("prm")
    g_mix = lambda l: prm[:, l, 0:8]
    g_ffn = lambda l: prm[:, l, 8:16]
    g_ssd = lambda l: prm[:, l, 16:20]
    g_q = lambda l: prm[:, l, 20:22]
    cw = lambda l: prm[:, l, 24:56].rearrange("p (b k) -> p b k", k=4)
    cb = lambda l: prm[:, l, 56:64]
    dtb = lambda l: prm[:, l, 64:72]
    A_b = lambda l: prm[:, l, 72:80]
    D_b = lambda l: prm[:, l, 80:88]
    gkv = sb("gkv", [128, DEPTH, 128]); T_gkv = Tk("gkv")
    gfin = sb("gfin", [128, 1024]); T_gfin = Tk("gfin")
    wukT = sb("wukT", [64, DEPTH, 8, 128], BF16); T_wukT = Tk("wukT")
    wuvp = sb("wuvp", [128, DEPTH, 8, 128], BF16); T_wuvp = Tk("wuvp")

    banks = [nc.alloc_psum_tensor("bank%d" % i, [128, 512], F32).ap() for i in range(8)]
    T_bank = [Tk("bank%d" % i) for i in range(8)]
    gstate = {"g": 0, "a": 0, "pt": 0}

    def gbank(hold=False):
        cands = [i for i in range(4) if not T_bank[i].busy]
        assert cands, "no free PSUM bank"
        i = min(cands, key=lambda i: T_bank[i].stamp)
        T_bank[i].busy = True
        T_bank[i].hold = hold
        return banks[i], T_bank[i]

    def unhold(T):
        T.hold = False

    def abank_s():
        i = 4 + gstate["a"] % 2
        gstate["a"] += 1
        return banks[i], T_bank[i]

    bankO, T_bankO = banks[6], T_bank[6]
    bankZ, T_bankZ = banks[7], T_bank[7]

    nci = nc.allow_non_contiguous_dma(reason="small parameter gathers")
    nci.__enter__()

    S.dma(S.sp, cst, consts_d, writes=[T_cst])
    S.op(S.dve, lambda: V.tensor_copy(cstb[:, 0:128], identf), reads=[T_cst], writes=[T_cstb])
    S.op(S.dve, lambda: V.tensor_copy(cstb[:, 128:256], onesf), reads=[T_cst], writes=[T_cstb])
    S.op(S.pool, lambda: G.memset(epsb[:, 0:1], EPS), writes=[T_eps])
    S.op(S.pool, lambda: G.memset(epsb[:, 1:2], 1.0), writes=[T_eps])
    for j in range(4):
        S.op(S.pool, lambda j=j: G.memset(PTd[j], 0.0), writes=[T_PTd[j]])
    S.op(S.pool, lambda: G.memset(wuvp, 0.0), writes=[T_wuvp])
    S.op(S.pool, lambda: G.memset(prm, 0.0), writes=[T_prm])
    for l in range(DEPTH):
        S.dma(S.sp, prm[:, l, 0:8], norm_mix[l].rearrange("(k p) -> p k", p=128), writes=[T_prm])
        S.dma(S.sp, prm[:, l, 8:16], norm_ffn[l].rearrange("(k p) -> p k", p=128), writes=[T_prm])
        S.dma(S.sp, prm[:, l, 16:20], ssd_norm[l].rearrange("(k p) -> p k", p=128), writes=[T_prm])
        S.dma(S.sp, prm[:, l, 20:22], q_norm[l].rearrange("(k p) -> p k", p=128), writes=[T_prm])
        for k in range(4):
            S.dma(S.sp, prm[:, l, 24:56].rearrange("p (b k) -> p b k", k=4)[:, :, k],
                  conv_w[l, k].rearrange("(b p) -> p b", p=128), writes=[T_prm])
        S.dma(S.sp, prm[:, l, 56:64], conv_b[l].rearrange("(k p) -> p k", p=128), writes=[T_prm])
        S.dma(S.sp, prm[:, l, 64:72], dt_bias[l].partition_broadcast(128), writes=[T_prm])
        S.dma(S.sp, prm[:, l, 72:80], a_log[l].partition_broadcast(128), writes=[T_prm])
        S.dma(S.sp, prm[:, l, 80:88], d_skip[l].partition_broadcast(128), writes=[T_prm])
        S.dma(S.sp, gkv[:, l, :], kv_norm[l].partition_broadcast(128), writes=[T_gkv])
    S.dma(S.sp, gfin, norm_final.partition_broadcast(128), writes=[T_gfin])
    for l in range(DEPTH):
        S.op(S.act, lambda l=l: A.activation(out=prm[:, l, 72:80], in_=prm[:, l, 72:80], func=AF.Exp),
             reads=[T_prm], writes=[T_prm])
        S.op(S.dve, lambda l=l: V.tensor_scalar(out=prm[:, l, 72:80], in0=prm[:, l, 72:80], scalar1=-1.0,
                                                scalar2=None, op0=ALU.mult), reads=[T_prm], writes=[T_prm])
    for l in range(DEPTH):
        wv = wuvp[:, l].rearrange("p (i q) c -> p i q c", q=2)
        src = w_uv[l].rearrange("l (i q) d -> l i q d", q=2)
        S.dma(S.pool, wv[:, :, 0, 0:64], src[:, :, 0, :], writes=[T_wuvp])
        S.dma(S.pool, wv[:, :, 1, 64:128], src[:, :, 1, :], writes=[T_wuvp])
    for l in range(DEPTH):
        S.dma(S.sp, sttmp.rearrange("p a b -> p (a b)"), w_uk[l].rearrange("l h d -> l (h d)"), writes=[T_sttmp])
        for hp in range(2):
            bk, T_bk = gbank()
            for hq in range(4):
                h = hp * 4 + hq
                S.op(S.pe, lambda h=h, hq=hq, bk=bk: P.transpose(
                    bk[:64, hq * 128:(hq + 1) * 128], sttmp.rearrange("p a b -> p (a b)")[:, h * 64:(h + 1) * 64], identf),
                    reads=[T_sttmp, T_cst], writes=[T_bk], sig=(hq == 3))
            S.op(S.act, lambda l=l, hp=hp, bk=bk: A.copy(
                wukT[:, l, hp * 4:(hp + 1) * 4, :], bk[:64, :].rearrange("p (a b) -> p a b", a=4)),
                reads=[T_bk], writes=[T_wukT])

    def chunk(l, c):
        return scr[l, c]

    def cast_weights(l):
        wi = w_in[l].rearrange("(k p) n -> p k n", p=128)

        def c8(c, ncol):
            return chunk(l, c).rearrange("p (k n) -> p k n", k=8)[:, :, 0:ncol]
        S.dma(S.pool, c8(C_CQ, 256), wi[:, :, S_DT:S_Q], writes=[T_scr[l][C_CQ]])
        S.dma(S.pool, c8(C_X0, 512), wi[:, :, 512:1024], writes=[T_scr[l][C_X0]])
        S.dma(S.pool, chunk(l, C_DTKV).rearrange("p (k n) -> p k n", k=8)[:, :, 0:8], wi[:, :, S_XBC:S_DT],
              writes=[T_scr[l][C_DTKV]])
        S.dma(S.pool, chunk(l, C_DTKV).rearrange("p (k n) -> p k n", k=8)[:, :, 8:168], wi[:, :, S_Q:IN_PROJ],
              writes=[T_scr[l][C_DTKV]])
        S.dma(S.pool, c8(C_X1, 512), wi[:, :, 1024:1536], writes=[T_scr[l][C_X1]])
        S.dma(S.pool, c8(C_Z, 512), wi[:, :, 0:512], writes=[T_scr[l][C_Z]])
        uq = w_uq[l].rearrange("(k p) (h c) -> p k h c", p=128, c=96)
        cu = chunk(l, C_UQ)[:, 0:2048].rearrange("p (k n) -> p k n", k=2)
        for k in range(2):
            S.dma(S.pool, cu[:, k, 0:512].rearrange("p (h d) -> p h d", d=64), uq[:, k, :, 0:64],
                  writes=[T_scr[l][C_UQ]])
            S.dma(S.pool, cu[:, k, 512:768].rearrange("p (h r) -> p h r", r=32), uq[:, k, :, 64:96],
                  writes=[T_scr[l][C_UQ]])
            rb = cu[:, k, 768:1024].rearrange("p (h r) -> p h r", r=32)
            S.dma(S.pool, rb[:, :, 0:16], uq[:, k, :, 80:96], writes=[T_scr[l][C_UQ]])
            S.dma(S.pool, rb[:, :, 16:32], uq[:, k, :, 64:80], writes=[T_scr[l][C_UQ]])
        wo = w_out[l].rearrange("(k p) n -> p k n", p=128)
        S.dma(S.pool, c8(C_O0, 512), wo[:, :, 0:512], writes=[T_scr[l][C_O0]])
        S.dma(S.pool, c8(C_O1, 512), wo[:, :, 512:1024], writes=[T_scr[l][C_O1]])
        wg = w_gate[l].rearrange("(k p) n -> p k n", p=128)
        wu = w_up[l].rearrange("(k p) n -> p k n", p=128)
        for i in range(11):
            cg = chunk(l, C_GU + i).rearrange("p (j g k n) -> p j g k n", j=2, g=2, k=8)
            for j in range(2):
                fb = 2 * i + j
                S.dma(S.pool, cg[:, j, 0], wg[:, :, fb * 128:(fb + 1) * 128], writes=[T_scr[l][C_GU + i]])
                S.dma(S.pool, cg[:, j, 1], wu[:, :, fb * 128:(fb + 1) * 128], writes=[T_scr[l][C_GU + i]])
        wd = w_down[l].rearrange("(k p) n -> p k n", p=128)
        for hf in range(2):
            for c in range(3):
                nk = 8 if c < 2 else NFB - 16
                S.dma(S.pool, chunk(l, C_D + hf * 3 + c).rearrange("p (k n) -> p k n", k=8)[:, 0:nk, :],
                      wd[:, c * 8:c * 8 + nk, hf * 512:(hf + 1) * 512], writes=[T_scr[l][C_D + hf * 3 + c]])

    cast_weights(0)
    for l in range(DEPTH):
        for sq in range(SB):
            for k in range(3):
                S.dma(S.sp, convhist_s[l][:, :, sq, k], state_conv[l, sq, k].rearrange("(b p) -> p b", p=128),
                      writes=[T_convhist_s[l]])

    tiles = []
    for b in range(PB):
        for t in range(NTP):
            tiles.append(TileD(cfg, "p", b, t))
    if SB > 0:
        tiles.append(TileD(cfg, "s"))
    order = []
    for _ in tiles:
        for l in range(DEPTH):
            for c in [C_CQ, C_X0, C_DTKV, C_X1, C_Z, C_UQ, C_O0, C_O1] + list(range(C_GU, NCHUNK)):
                order.append((l, c))
    ws = {"issued": 0, "used": 0}

    def ws_issue():
        i = ws["issued"]
        if i >= len(order):
            return
        l, c = order[i]
        slot = i % NSLOT
        dst, src = wring[slot], scr[l, c]
        if c == C_DTKV:
            dst, src = [a.rearrange("p (k n) -> p k n", k=8)[:, :, 0:168] for a in (dst, src)]
        elif c == C_CQ:
            dst, src = [a.rearrange("p (k n) -> p k n", k=8)[:, :, 0:256] for a in (dst, src)]
        elif c == C_UQ:
            dst, src = dst[:, 0:2048], src[:, 0:2048]
        elif c in (C_D + 2, C_D + 5):
            dst, src = [a.rearrange("p (k n) -> p k n", k=8)[:, 0:NFB - 16, :] for a in (dst, src)]
        S.dma(S.sp, dst, src, reads=[T_scr[l][c]], writes=[T_wring[slot]])
        ws["issued"] = i + 1

    def ws_get(l, c, live_prev=0):
        i = ws["used"]
        assert order[i] == (l, c), (order[i], l, c)
        while ws["issued"] < min(i + NSLOT - live_prev, len(order)):
            ws_issue()
        ws["used"] = i + 1
        return wring[i % NSLOT], T_wring[i % NSLOT]

    flip = {"ev": 0}

    def evac_eng():
        flip["ev"] ^= 1
        return flip["ev"]

    def norm_stats(td, s):
        PS = td.PS
        st, T_st = nst[s], T_nst[s]
        S.op(S.act, lambda: A.activation(out=junk[:PS, :], in_=xt[:PS, s, :], func=AF.Square, accum_out=st[:PS, 0:1]),
             reads=[T_xts[s]], writes=[T_junk, T_st])
        S.op(S.act, lambda: A.activation(out=st[:PS, 1:2], in_=st[:PS, 0:1], func=AF.Ln, scale=1.0 / D_MODEL, bias=epsb[:PS, 0:1]),
             reads=[T_st, T_eps], writes=[T_st])
        S.op(S.act, lambda: A.activation(out=st[:PS, 2:3], in_=st[:PS, 1:2], func=AF.Exp, scale=-0.5), reads=[T_st], writes=[T_st])
        S.op(S.dve, lambda: V.tensor_scalar(out=xn[:PS, s, :], in0=xt[:PS, s, :], scalar1=st[:PS, 2:3], scalar2=None, op0=ALU.mult),
             reads=[T_xts[s], T_st], writes=[T_PHB])

    def norm_tr(td, s, gains):
        PS = td.PS
        bk, T_bk = gbank()
        reg = bk.bitcast(BF16)
        for kt in range(8):
            S.op(S.pe, lambda kt=kt: P.transpose(reg[:, kt * PS:(kt + 1) * PS], xn[:PS, s, kt * 128:(kt + 1) * 128], identb[:PS, :PS]),
                 reads=[T_PHB, T_cstb], writes=[T_bk], sig=(kt == 7))
        S.op(S.dve, lambda: V.tensor_tensor(out=hT[:, :, s * PS:(s + 1) * PS], in0=reg[:, 0:8 * PS].rearrange("p (k c) -> p k c", k=8),
                                            in1=gains.unsqueeze(2).broadcast_to([128, 8, PS]), op=ALU.mult),
             reads=[T_bk, T_prm], writes=[T_hT])

    def rms_to_hT(td, gains):
        for s in range(td.S):
            norm_stats(td, s)
        for s in range(td.S):
            norm_tr(td, s, gains)

    def in_proj_cq(l, td):
        PS, Sn, NT, NSEG, LSEG = td.PS, td.S, td.NT, td.NSEG, td.LSEG
        wq, T_wq = ws_get(l, C_CQ)
        wq = wq.rearrange("p (k n) -> p k n", k=8)
        cqb = []
        for m in range(2):
            bk, T_bk = gbank(hold=True)
            cqb.append((bk, T_bk))
            for kt in range(8):
                S.op(S.pe, lambda kt=kt, bk=bk, m=m: P.matmul(bk[:, :NT], wq[:, kt, m * 128:(m + 1) * 128], hT[:, kt, :NT],
                                                              start=(kt == 0), stop=(kt == 7)),
                     reads=[T_hT, T_wq], writes=[T_bk], sig=(kt == 7))
            S.op(S.act, lambda bk=bk, m=m: A.activation(out=cq2[:, m, :NT], in_=bk[:, :NT], func=AF.Square),
                 reads=[T_bk], writes=[T_PHB])
        bs, T_bs = gbank()
        for m in range(2):
            S.op(S.pe, lambda m=m, bs=bs: P.matmul(bs[:, :NT], onesb, cq2[:, m, :NT], start=(m == 0), stop=(m == 1)),
                 reads=[T_PHB, T_cstb], writes=[T_bs], sig=(m == 1))
        S.op(S.act, lambda bs=bs: A.activation(out=sqtmp[:, :NT], in_=bs[:, :NT], func=AF.Ln, scale=1.0 / Q_LORA,
                                               bias=epsb[:, 0:1]), reads=[T_bs, T_eps], writes=[T_PHB])
        S.op(S.act, lambda: A.activation(out=rstdb[:, :NT], in_=sqtmp[:, :NT], func=AF.Exp, scale=-0.5), reads=[T_PHB], writes=[T_PHB])
        for m in range(2):
            bk, T_bk = cqb[m]
            unhold(T_bk)
            S.op(S.dve, lambda bk=bk, m=m: V.scalar_tensor_tensor(out=cqn[:, m, :NT], in0=bk[:, :NT], scalar=g_q(l)[:, m:m + 1],
                                                                  in1=rstdb[:, :NT], op0=ALU.mult, op1=ALU.mult),
                 reads=[T_bk, T_PHB, T_prm], writes=[T_PHB])


    def proj_z(l, td, s, wz, T_wz):
        PS = td.PS
        bk, T_bk = gbank()
        for kt in range(8):
            S.op(S.pe, lambda kt=kt: P.matmul(bk[:PS, :], hT[:, kt, s * PS:(s + 1) * PS], wz[:, kt, :], start=(kt == 0), stop=(kt == 7)),
                 reads=[T_hT, T_wz], writes=[T_bk], sig=(kt == 7))
        S.op(S.act, lambda: A.activation(out=zs[:PS, s, :], in_=bk[:PS, :], func=AF.Silu), reads=[T_bk], writes=[T_zs])

    def proj_dtkv(l, td, s, wk, T_wk):
        PS = td.PS
        bk, T_bk = gbank()
        for kt in range(8):
            S.op(S.pe, lambda kt=kt: P.matmul(bk[:PS, 0:168], hT[:, kt, s * PS:(s + 1) * PS], wk[:, kt, 0:168],
                                              start=(kt == 0), stop=(kt == 7)),
                 reads=[T_hT, T_wk], writes=[T_bk], sig=(kt == 7))
        S.op(S.dve, lambda: V.tensor_copy(tk[:PS, s, :], bk[:PS, 0:168]), reads=[T_bk], writes=[T_tk])

    def xbc_block(l, td, blk, wx, T_wx, cs):
        PS, Sn, NT, NSEG, LSEG = td.PS, td.S, td.NT, td.NSEG, td.LSEG
        bk, T_bk = gbank()
        wc = (blk % 4) * 128
        for kt in range(8):
            S.op(S.pe, lambda kt=kt: P.matmul(bk[:, :NT], wx[:, kt, wc:wc + 128], hT[:, kt, :NT], start=(kt == 0), stop=(kt == 7)),
                 reads=[T_hT, T_wx], writes=[T_bk], sig=(kt == 7))
        u, T_u = utmp[blk % NCB], T_utmp[blk % NCB]
        uv = u[:, 0:NSEG * (LSEG + 3)].rearrange("p (g c) -> p g c", g=NSEG)
        ac, T_ac = cacc[blk % NCB], T_cacc[blk % NCB]
        acv = ac[:, 0:NT].rearrange("p (g c) -> p g c", g=NSEG)
        chT, T_ch = (convhist[l], T_convhist[l]) if td.kind == "p" else (convhist_s[l], T_convhist_s[l])
        ch = chT[:, blk, 0:NSEG, :]
        S.op(S.pool, lambda: G.tensor_copy(uv[:, :, 0:3], ch), reads=[T_ch], writes=[T_u])
        S.op(S.act, lambda: A.copy(uv[:, :, 3:3 + LSEG], bk[:, :NT].rearrange("p (g c) -> p g c", g=NSEG)), reads=[T_bk], writes=[T_u])
        S.op(S.pool, lambda: G.tensor_copy(ch, uv[:, :, LSEG:LSEG + 3]), reads=[T_u], writes=[T_ch])
        cwl = cw(l)
        S.op(S.pool, lambda: G.tensor_scalar(out=acv, in0=uv[:, :, 0:LSEG], scalar1=cwl[:, blk, 0:1], scalar2=cb(l)[:, blk:blk + 1],
                                             op0=ALU.mult, op1=ALU.add), reads=[T_u, T_prm], writes=[T_ac])
        for k in range(1, 4):
            S.op(S.dve, lambda k=k: V.scalar_tensor_tensor(out=acv, in0=uv[:, :, k:k + LSEG], scalar=cwl[:, blk, k:k + 1], in1=acv,
                                                           op0=ALU.mult, op1=ALU.add), reads=[T_u, T_prm, T_ac], writes=[T_ac])
        first = cs["first"]
        cs["first"] = False

        def silu_out():
            wl = [T_convout, T_hmid] if first else [T_convout]
            S.op(S.act, lambda: A.activation(out=convout[:, blk, :NT], in_=ac[:, :NT], func=AF.Silu), reads=[T_ac], writes=wl)
        if cs["pend"] is not None:
            cs["pend"]()
        cs["pend"] = silu_out

    def small_token(l, td, tbl_row0):
        PS, Sn = td.PS, td.S
        x0 = dtmp[:PS, 0, 0:Sn * 8].rearrange("p (s h) -> p s h", s=Sn)
        x1 = dtmp[:PS, 1, 0:Sn * 8].rearrange("p (s h) -> p s h", s=Sn)
        x2 = dtmp[:PS, 2, 0:Sn * 8].rearrange("p (s h) -> p s h", s=Sn)
        S.op(S.dve, lambda: V.tensor_tensor(out=x0, in0=tk[:PS, 0:Sn, 0:8], in1=dtb(l)[:PS].unsqueeze(1).broadcast_to([PS, Sn, 8]),
                                            op=ALU.add), reads=[T_tk, T_prm], writes=[T_dtmp])
        S.op(S.act, lambda: A.activation(out=x1, in_=x0, func=AF.Abs), reads=[T_dtmp], writes=[T_dtmp])
        S.op(S.act, lambda: A.activation(out=x1, in_=x1, func=AF.Exp, scale=-1.0), reads=[T_dtmp], writes=[T_dtmp])
        S.op(S.act, lambda: A.activation(out=x2, in_=x1, func=AF.Ln, bias=epsb[:PS, 1:2]), reads=[T_dtmp, T_eps], writes=[T_dtmp])
        S.op(S.dve, lambda: V.scalar_tensor_tensor(out=dtt[:PS, 0:Sn, :], in0=x0, scalar=0.0, in1=x2, op0=ALU.max, op1=ALU.add),
             reads=[T_dtmp], writes=[T_dtt])
        S.op(S.dve, lambda: V.tensor_tensor(out=att[:PS, 0:Sn, :], in0=dtt[:PS, 0:Sn, :],
                                            in1=A_b(l)[:PS].unsqueeze(1).broadcast_to([PS, Sn, 8]), op=ALU.mult),
             reads=[T_dtt, T_prm], writes=[T_att])
        for s in range(Sn):
            S.op(S.act, lambda s=s: A.activation(out=junk[:PS, 0:128], in_=tk[:PS, s, 8:136], func=AF.Square,
                                                 accum_out=nstat[:PS, 12 + s:13 + s]),
                 reads=[T_tk], writes=[T_junk, T_nstat])
        S.op(S.act, lambda: A.activation(out=nstat[:PS, 12:12 + Sn], in_=nstat[:PS, 12:12 + Sn], func=AF.Ln,
                                         scale=1.0 / KV_LORA, bias=epsb[:PS, 0:1]), reads=[T_nstat, T_eps], writes=[T_nstat])
        S.op(S.act, lambda: A.activation(out=nstat[:PS, 12:12 + Sn], in_=nstat[:PS, 12:12 + Sn], func=AF.Exp, scale=-0.5),
             reads=[T_nstat], writes=[T_nstat])
        for s in range(Sn):
            S.op(S.dve, lambda s=s: V.scalar_tensor_tensor(out=ckvn[:PS, s, :], in0=tk[:PS, s, 8:136],
                                                           scalar=nstat[:PS, 12 + s:13 + s], in1=gkv[:PS, l, :],
                                                           op0=ALU.mult, op1=ALU.mult),
                 reads=[T_tk, T_nstat, T_gkv], writes=[T_ckvn])
        S.op(S.act, lambda: A.copy(ckvb[:PS, 0:Sn, :], ckvn[:PS, 0:Sn, :]), reads=[T_ckvn], writes=[T_ckvb])
        cos = ropetm[:PS, 0:Sn, 0:16]
        sin = ropetm[:PS, 0:Sn, 16:32]
        r1 = tk[:PS, 0:Sn, 136:152]
        r2 = tk[:PS, 0:Sn, 152:168]
        t0 = dtmp[:PS, 3, 0:Sn * 16].rearrange("p (s h) -> p s h", s=Sn)
        t1 = dtmp[:PS, 4, 0:Sn * 16].rearrange("p (s h) -> p s h", s=Sn)
        S.op(S.dve, lambda: V.tensor_tensor(out=t0, in0=r1, in1=cos, op=ALU.mult), reads=[T_tk, T_ropetm], writes=[T_dtmp])
        S.op(S.dve, lambda: V.tensor_tensor(out=t1, in0=r2, in1=sin, op=ALU.mult), reads=[T_tk, T_ropetm], writes=[T_dtmp])
        S.op(S.dve, lambda: V.tensor_tensor(out=krn[:PS, 0:Sn, 0:16], in0=t0, in1=t1, op=ALU.subtract),
             reads=[T_dtmp], writes=[T_krn])
        S.op(S.dve, lambda: V.tensor_tensor(out=t0, in0=r1, in1=sin, op=ALU.mult), reads=[T_tk, T_ropetm], writes=[T_dtmp])
        S.op(S.dve, lambda: V.tensor_tensor(out=t1, in0=r2, in1=cos, op=ALU.mult), reads=[T_tk, T_ropetm], writes=[T_dtmp])
        S.op(S.dve, lambda: V.tensor_tensor(out=krn[:PS, 0:Sn, 16:32], in0=t0, in1=t1, op=ALU.add),
             reads=[T_dtmp], writes=[T_krn])
        for c in range(4):
            S.op(S.act, lambda c=c: A.copy(kr4b[:PS, 0:Sn, c * 32:(c + 1) * 32], krn[:PS, 0:Sn, :]), reads=[T_krn], writes=[T_kr4b])
        if td.kind == "p":
            r0 = td.t * 512
            S.dma(S.pool, o_ckv_p[l, td.b, r0:r0 + 512, :].rearrange("(s p) c -> p s c", p=128), ckvn[:, 0:4, :], reads=[T_ckvn])
            S.dma(S.pool, o_kr_p[l, td.b, r0:r0 + 512, :].rearrange("(s p) c -> p s c", p=128), krn[:, 0:4, :], reads=[T_krn])
        else:
            S.dma(S.pool, o_ckv_s[l].rearrange("s p c -> p s c"), ckvn[:PS, 0:Sn, :], reads=[T_ckvn])
            S.dma(S.pool, o_kr_s[l].rearrange("s p c -> p s c"), krn[:PS, 0:Sn, :], reads=[T_krn])

    def kv_new(l, td, s, ktile, PSk):
        col = ktile * 128
        bk, T_bk = gbank()
        reg = bk.bitcast(BF16)
        S.op(S.pe, lambda: P.transpose(reg[:, 0:PSk], ckvb[:PSk, s, :], identb[:PSk, :PSk]),
             reads=[T_ckvb, T_cstb], writes=[T_bk], sig=False)
        S.op(S.pe, lambda: P.transpose(reg[:, 128:128 + PSk], kr4b[:PSk, s, :], identb[:PSk, :PSk]),
             reads=[T_kr4b, T_cstb], writes=[T_bk])
        S.op(S.act, lambda: A.copy(ckvT[l][:, col:col + PSk], reg[:, 0:PSk]), reads=[T_bk], writes=[T_cache[l]])
        S.op(S.dve, lambda: V.tensor_copy(kr4T[l][:, col:col + PSk], reg[:, 128:128 + PSk]), reads=[T_bk], writes=[T_cache[l]])
        S.op(S.pool, lambda: G.tensor_copy(ckvtok[l][:PSk, ktile, :], ckvb[:PSk, s, :]), reads=[T_ckvb], writes=[T_cache[l]])

    def kv_past(l, sq):
        S.dma(S.pool, pastb, cache_ckv[l, sq].rearrange("(k p) c -> p k c", p=128), writes=[T_pastb])
        S.dma(S.pool, pastkr, cache_kr[l, sq].rearrange("(k p) c -> p k c", p=128), writes=[T_pastkr])
        for c in range(4):
            S.op(S.dve, lambda c=c: V.tensor_copy(pastkb[:, :, c * 32:(c + 1) * 32], pastkr), reads=[T_pastkr], writes=[T_pastkb])
        S.op(S.pool, lambda: G.tensor_copy(ckvtok[l][:, 0:NPK, :], pastb), reads=[T_pastb], writes=[T_cache[l]])
        for k in range(NPK):
            bk, T_bk = gbank()
            reg = bk.bitcast(BF16)
            S.op(S.pe, lambda k=k, reg=reg: P.transpose(reg[:, 0:128], pastb[:, k, :], identb), reads=[T_pastb, T_cstb],
                 writes=[T_bk], sig=False)
            S.op(S.pe, lambda k=k, reg=reg: P.transpose(reg[:, 128:256], pastkb[:, k, :], identb), reads=[T_pastkb, T_cstb],
                 writes=[T_bk])
            S.op(S.act, lambda k=k, reg=reg: A.copy(ckvT[l][:, k * 128:(k + 1) * 128], reg[:, 0:128]), reads=[T_bk],
                 writes=[T_cache[l]])
            S.op(S.dve, lambda k=k, reg=reg: V.tensor_copy(kr4T[l][:, k * 128:(k + 1) * 128], reg[:, 128:256]), reads=[T_bk],
                 writes=[T_cache[l]])

    def ssd_sub(l, td, s):
        L = td.PS
        c0 = s * L
        XB, T_XB = XBt[s % 2], T_XBt[s % 2]
        ex, T_ex = ex2[s % 2], T_ex2[s % 2]
        a_s = att[:L, s, :]
        bk, T_bk = gbank()
        reg = bk.bitcast(BF16)
        for q in range(6):
            S.op(S.pe, lambda q=q: P.transpose(reg[:L, q * 128:(q + 1) * 128], convout[:, q, c0:c0 + L], identb),
                 reads=[T_convout, T_cstb], writes=[T_bk], sig=(q == 5))
        S.op(S.act, lambda: A.copy(XB[:L, :], reg[:L, 0:768]), reads=[T_bk], writes=[T_XB])
        S.op(S.pool, lambda: G.tensor_tensor(out=Rt[:L, :, :L], in0=Uf[:L, :L].unsqueeze(1).broadcast_to([L, 8, L]),
                                             in1=a_s.unsqueeze(2).broadcast_to([L, 8, L]), op=ALU.mult),
             reads=[T_cst, T_att], writes=[T_Rt])
        bm, T_bm = gbank()
        S.op(S.pe, lambda: P.matmul(bm[:L, 0:8], Uf[:L, :L], a_s, start=True, stop=True, skip_group_check=True),
             reads=[T_cst, T_att], writes=[T_bm], sig=False)
        S.op(S.pe, lambda: P.matmul(bm[:L, 8:16], Mstf[:L, :L], a_s, start=True, stop=True, skip_group_check=True),
             reads=[T_cst, T_att], writes=[T_bm], sig=False)
        S.op(S.pe, lambda: P.matmul(bm[:, 16:24], onesf[:L, :], a_s, start=True, stop=True, skip_group_check=True),
             reads=[T_cst, T_att], writes=[T_bm])
        S.op(S.act, lambda: A.activation(out=ex[:L, 0:16], in_=bm[:L, 0:16], func=AF.Exp), reads=[T_bm], writes=[T_ex])
        S.op(S.act, lambda: A.activation(out=ex[:, 16:24], in_=bm[:, 16:24], func=AF.Exp), reads=[T_bm], writes=[T_ex])
        bc, T_bc = gbank()
        for g in range(2):
            S.op(S.pe, lambda g=g: P.matmul(bc[:L, g * L:(g + 1) * L], convout[:, 4 + g, c0:c0 + L],
                                            convout[:, 6 + g, c0:c0 + L], start=True, stop=True, skip_group_check=True),
                 reads=[T_convout], writes=[T_bc], sig=(g == 1))
        yield
        S.op(S.dve, lambda: V.tensor_tensor(out=ex[:L, 24:32], in0=ex[:L, 8:16], in1=dtt[:L, s, :], op=ALU.mult),
             reads=[T_ex, T_dtt], writes=[T_ex])
        S.op(S.dve, lambda: V.tensor_tensor(out=CBm[:L, :, :L], in0=bc[:L, 0:2 * L].rearrange("p (g c) -> p g c", g=2),
                                            in1=Uf[:L, :L].unsqueeze(1).broadcast_to([L, 2, L]), op=ALU.mult),
             reads=[T_bc, T_cst], writes=[T_CBm])
        segb = []
        for hf in range(2):
            bs, T_bs = gbank()
            segb.append((bs, T_bs))
            S.op(S.pe, lambda hf=hf, bs=bs: P.matmul(bs[:L, 0:4 * L].rearrange("p (h c) -> p h c", h=4), Mstf[:L, :L],
                                                     Rt[:L, hf * 4:(hf + 1) * 4, :L], start=True, stop=True),
                 reads=[T_cst, T_Rt], writes=[T_bs])
        yield
        for hf in range(2):
            bs, T_bs = segb[hf]
            S.op(S.act, lambda hf=hf, bs=bs: A.activation(out=Et[:L, hf * 4:(hf + 1) * 4, :L],
                                                          in_=bs[:L, 0:4 * L].rearrange("p (h c) -> p h c", h=4), func=AF.Exp),
                 reads=[T_bs], writes=[T_Et])
        S.op(S.pool, lambda: G.tensor_tensor(out=xD[:L, :].rearrange("p (h d) -> p h d", h=8),
                                             in0=XB[:L, 0:512].rearrange("p (h d) -> p h d", h=8),
                                             in1=D_b(l)[:L].unsqueeze(2).broadcast_to([L, 8, 64]), op=ALU.mult),
             reads=[T_XB, T_prm], writes=[T_xD])
        S.op(S.pool, lambda: G.tensor_tensor(out=xw[:L, :].rearrange("p (h d) -> p h d", h=8),
                                             in0=XB[:L, 0:512].rearrange("p (h d) -> p h d", h=8),
                                             in1=ex[:L, 24:32].unsqueeze(2).broadcast_to([L, 8, 64]), op=ALU.mult),
             reads=[T_XB, T_ex], writes=[T_xw])
        yield
        for g in range(2):
            S.op(S.dve, lambda g=g: V.tensor_tensor(out=Et[:L, g * 4:(g + 1) * 4, :L], in0=Et[:L, g * 4:(g + 1) * 4, :L],
                                                    in1=CBm[:L, g, :L].unsqueeze(1).broadcast_to([L, 4, L]), op=ALU.mult),
                 reads=[T_Et, T_CBm], writes=[T_Et])
        S.op(S.dve, lambda: V.tensor_tensor(out=Wb[:L, :, :L], in0=Et[:L, :, :L],
                                            in1=dtt[:L, s, :].unsqueeze(2).broadcast_to([L, 8, L]), op=ALU.mult),
             reads=[T_Et, T_dtt], writes=[T_Wb])
        yield
        by, T_by = gbank()
        for h in range(8):
            S.op(S.pe, lambda h=h: P.matmul(by[:L, h * 64:(h + 1) * 64], Wb[:L, h, :L], XB[:L, h * 64:(h + 1) * 64],
                                            start=(h == 0), stop=False, skip_group_check=True),
                 reads=[T_Wb, T_XB], writes=[T_by], sig=False)
        S.op(S.pe, lambda: P.matmul(by[:L, :], identb[:L, :L], xD[:L, :], start=False, stop=True, skip_group_check=True),
             reads=[T_xD, T_cstb], writes=[T_by])
        bi, T_bi = gbank()
        for g in range(2):
            S.op(S.pe, lambda g=g: P.matmul(bi[:L, g * 256:(g + 1) * 256], convout[:, 6 + g, c0:c0 + L],
                                            hTb[l][:, g * 256:(g + 1) * 256], start=(g == 0), stop=(g == 1),
                                            skip_group_check=True),
                 reads=[T_convout, T_hT_st[l]], writes=[T_bi], sig=(g == 1))
        yield
        S.op(S.dve, lambda: V.tensor_tensor(out=ytmp[:L, :].rearrange("p (h d) -> p h d", h=8),
                                            in0=bi[:L, :].rearrange("p (h d) -> p h d", h=8),
                                            in1=ex[:L, 0:8].unsqueeze(2).broadcast_to([L, 8, 64]), op=ALU.mult),
             reads=[T_bi, T_ex], writes=[T_ytmp])
        S.op(S.dve, lambda: V.tensor_tensor(out=ytmp[:L, :], in0=ytmp[:L, :], in1=by[:L, :], op=ALU.add),
             reads=[T_by, T_ytmp], writes=[T_ytmp])
        S.op(S.dve, lambda: V.tensor_tensor(out=yg[:L, :], in0=ytmp[:L, :], in1=zs[:L, s, :], op=ALU.mult),
             reads=[T_ytmp, T_zs], writes=[T_yg])
        bh, T_bh = gbank()
        for g in range(2):
            S.op(S.pe, lambda g=g: P.matmul(bh[:, g * 256:(g + 1) * 256], XB[:L, 512 + g * 128:512 + (g + 1) * 128],
                                            xw[:L, g * 256:(g + 1) * 256], start=(g == 0), stop=(g == 1),
                                            skip_group_check=True),
                 reads=[T_XB, T_xw], writes=[T_bh], sig=(g == 1))
        yield
        S.op(S.dve, lambda: V.tensor_tensor(out=hTf[l].rearrange("p (h d) -> p h d", h=8),
                                            in0=hTf[l].rearrange("p (h d) -> p h d", h=8),
                                            in1=ex[:, 16:24].unsqueeze(2).broadcast_to([128, 8, 64]), op=ALU.mult),
             reads=[T_ex, T_hT_st[l]], writes=[T_hT_st[l]])
        S.op(S.dve, lambda: V.tensor_tensor(out=hTf[l], in0=hTf[l], in1=bh, op=ALU.add),
             reads=[T_bh, T_hT_st[l]], writes=[T_hT_st[l]])
        S.op(S.act, lambda: A.activation(out=junk[:L, 0:512], in_=yg[:L, :], func=AF.Square, accum_out=nstat[:L, 0:1]),
             reads=[T_yg], writes=[T_junk, T_nstat])
        S.op(S.act, lambda: A.activation(out=nstat[:L, 1:2], in_=nstat[:L, 0:1], func=AF.Ln, scale=1.0 / 512,
                                         bias=epsb[:L, 0:1]), reads=[T_nstat, T_eps], writes=[T_nstat])
        yield
        S.op(S.act, lambda: A.copy(hTb[l], hTf[l]), reads=[T_hT_st[l]], writes=[T_hT_st[l]])
        S.op(S.act, lambda: A.activation(out=nstat[:L, 2:3], in_=nstat[:L, 1:2], func=AF.Exp, scale=-0.5), reads=[T_nstat], writes=[T_nstat])
        S.op(S.act, lambda: A.activation(out=mixtok[:L, s, :], in_=yg[:L, :], func=AF.Copy, scale=nstat[:L, 2:3]),
             reads=[T_yg, T_nstat], writes=[T_mixtok])
        yield

    def state_zero(l):
        S.op(S.pool, lambda: G.memset(hTf[l], 0.0), writes=[T_hT_st[l]])
        S.op(S.pool, lambda: G.memset(hTb[l], 0.0), writes=[T_hT_st[l]])

    def state_load(l, sq):
        S.dma(S.pool, sttmp, state_ssm[l, sq].rearrange("h p n -> (h p) n").rearrange("(q r) n -> r q n", r=128), writes=[T_sttmp])
        bk, T_bk = gbank()
        for q in range(4):
            S.op(S.pe, lambda q=q, bk=bk: P.transpose(bk[:, q * 128:(q + 1) * 128], sttmp[:, q, :], identf),
                 reads=[T_sttmp, T_cst], writes=[T_bk], sig=(q == 3))
        S.op(S.dve, lambda bk=bk: V.tensor_copy(hTf[l], bk), reads=[T_bk], writes=[T_hT_st[l]])
        S.op(S.act, lambda bk=bk: A.copy(hTb[l], bk), reads=[T_bk], writes=[T_hT_st[l]])

    def state_store(l, dst):
        bk, T_bk = gbank()
        for q in range(4):
            S.op(S.pe, lambda q=q, bk=bk: P.transpose(bk[:, q * 128:(q + 1) * 128], hTf[l][:, q * 128:(q + 1) * 128], identf),
                 reads=[T_hT_st[l], T_cst], writes=[T_bk], sig=(q == 3))
        S.op(S.dve, lambda bk=bk: V.tensor_copy(sttmp, bk.rearrange("p (q n) -> p q n", q=4)), reads=[T_bk], writes=[T_sttmp])
        S.dma(S.pool, dst.rearrange("h p n -> (h p) n").rearrange("(q r) n -> r q n", r=128), sttmp, reads=[T_sttmp])

    def q_rope(l, td, wq, T_wq):
        NT = td.NT
        for g in range(2):
            ba, T_ba = gbank()
            bb, T_bb = gbank()
            for kt in range(2):
                S.op(S.pe, lambda kt=kt, g=g, ba=ba: P.matmul(ba[:, :NT], wq[:, kt, 512 + g * 128:512 + (g + 1) * 128],
                                                              cqn[:, kt, :NT], start=(kt == 0), stop=(kt == 1)),
                     reads=[T_PHB, T_wq], writes=[T_ba], sig=(kt == 1))
            for kt in range(2):
                S.op(S.pe, lambda kt=kt, g=g, bb=bb: P.matmul(bb[:, :NT], wq[:, kt, 768 + g * 128:768 + (g + 1) * 128],
                                                              cqn[:, kt, :NT], start=(kt == 0), stop=(kt == 1)),
                     reads=[T_PHB, T_wq], writes=[T_bb], sig=(kt == 1))
            S.op(S.dve, lambda ba=ba: V.tensor_tensor(out=ropet[0][:, :NT], in0=ba[:, :NT], in1=ropefm[:, 0, :NT], op=ALU.mult),
                 reads=[T_ba, T_ropefm], writes=[T_ropet[0]])
            S.op(S.dve, lambda bb=bb: V.tensor_tensor(out=ropet[1][:, :NT], in0=bb[:, :NT], in1=ropefm[:, 1, :NT], op=ALU.mult),
                 reads=[T_bb, T_ropefm], writes=[T_ropet[1]])
            S.op(S.pool, lambda g=g: G.tensor_tensor(out=qr4[g][:, :NT], in0=ropet[0][:, :NT], in1=ropet[1][:, :NT], op=ALU.add),
                 reads=[T_ropet[0], T_ropet[1]], writes=[T_qr4[g]])

    def q_head(l, td, h, wq, T_wq):
        NT = td.NT
        i = h % 2
        bn, T_bn = gbank()
        for kt in range(2):
            S.op(S.pe, lambda kt=kt, bn=bn: P.matmul(bn[:64, :NT], wq[:, kt, h * 64:(h + 1) * 64], cqn[:, kt, :NT],
                                                     start=(kt == 0), stop=(kt == 1)),
                 reads=[T_PHB, T_wq], writes=[T_bn], sig=(kt == 1))
        S.op(S.act, lambda bn=bn: A.copy(qn_h[i][:, :NT], bn[:64, :NT]), reads=[T_bn], writes=[T_qn[i]])
        bl, T_bl = gbank()
        S.op(S.pe, lambda bl=bl: P.matmul(bl[:, :NT], wukT[:, l, h, :], qn_h[i][:, :NT], start=True, stop=True),
             reads=[T_qn[i], T_wukT], writes=[T_bl])
        S.op(S.dve, lambda bl=bl: V.tensor_copy(qlat_h[i][:, :NT], bl[:, :NT]), reads=[T_bl], writes=[T_qlat[i]])
        g, hh = h // 4, h % 4
        S.op(S.act, lambda: A.activation(out=qrz[i][:, :NT], in_=qr4[g][:, :NT], func=AF.Copy, scale=hmask[:, hh:hh + 1]),
             reads=[T_qr4[g], T_cst], writes=[T_qrz[i]])

    def attn_head(l, td, h, q0, nq, keytiles, fin_box):
        i = h % 2
        nk = len(keytiles)

        def scores(ki):
            kt, nkeys, c0, dj = keytiles[ki]
            bs, T_bs = abank_s()
            a0, a1 = q0 + c0, q0 + nq
            col = kt * 128
            S.op(S.pe, lambda: P.matmul(bs[:nkeys, a0:a1], ckvT[l][:, col:col + nkeys], qlat_h[i][:, a0:a1], start=True, stop=False),
                 reads=[T_cache[l], T_qlat[i]], writes=[T_bs], sig=False)
            S.op(S.pe, lambda: P.matmul(bs[:nkeys, a0:a1], kr4T[l][:, col:col + nkeys], qrz[i][:, a0:a1], start=False, stop=True),
                 reads=[T_cache[l], T_qrz[i]], writes=[T_bs])
            if dj is None:
                idx = gstate["pt"] % NPT
                gstate["pt"] += 1
                pt, T_pt = PTf[idx], T_PTf[idx]
                S.op(S.act, lambda: A.activation(out=pt[:nkeys, a0:a1], in_=bs[:nkeys, a0:a1], func=AF.Exp, scale=SCALE),
                     reads=[T_bs], writes=[T_pt])
            else:
                pt, T_pt = PTd[dj], T_PTd[dj]
                S.op(S.act, lambda: A.activation(out=pt[:, a0 + 64:a1], in_=bs[:, a0 + 64:a1], func=AF.Exp, scale=SCALE),
                     reads=[T_bs], writes=[T_pt])
                S.op(S.act, lambda: A.activation(out=pt[0:64, a0:a0 + 64], in_=bs[0:64, a0:a0 + 64], func=AF.Exp, scale=SCALE),
                     reads=[T_bs], writes=[T_pt])
            return pt, T_pt

        def pv(ki, pt, T_pt):
            kt, nkeys, c0, dj = keytiles[ki]
            a0, a1 = q0 + c0, q0 + nq
            S.op(S.pe, lambda: P.matmul(bankO[:, a0:a1], ckvtok[l][:nkeys, kt, :], pt[:nkeys, a0:a1], start=(ki == 0),
                                        stop=(ki == nk - 1), skip_group_check=True),
                 reads=[T_cache[l], T_pt], writes=[T_bankO], sig=False)
            S.op(S.pe, lambda: P.matmul(bankZ[:, a0:a1], onesb[:nkeys, :], pt[:nkeys, a0:a1], start=(ki == 0),
                                        stop=(ki == nk - 1), skip_group_check=True),
                 reads=[T_cstb, T_pt], writes=[T_bankZ], sig=(ki == nk - 1))

        pend = None
        prev_fin = fin_box["fin"]
        for ki in range(nk):
            cur = scores(ki)
            if ki == 1 and prev_fin is not None:
                prev_fin()
                prev_fin = None
            if pend is not None:
                pv(ki - 1, *pend)
            pend = cur
            yield
        if prev_fin is not None:
            prev_fin()
        pv(nk - 1, *pend)
        b0, b1 = q0, q0 + nq
        S.op(S.act, lambda: A.activation(out=rz[:, b0:b1], in_=bankZ[:, b0:b1], func=AF.Ln), reads=[T_bankZ], writes=[T_rz])
        S.op(S.act, lambda: A.activation(out=rz[:, b0:b1], in_=rz[:, b0:b1], func=AF.Exp, scale=-1.0), reads=[T_rz], writes=[T_rz])
        ol, T_ol = olat2[i], T_olat2[i]
        S.op(S.dve, lambda: V.tensor_tensor(out=ol[:, b0:b1], in0=bankO[:, b0:b1], in1=rz[:, b0:b1], op=ALU.mult),
             reads=[T_bankO, T_rz], writes=[T_ol])

        def fin():
            bg, T_bg = gbank()
            S.op(S.pe, lambda: P.matmul(bg[:, b0:b1], wuvp[:, l, h, :], ol[:, b0:b1], start=True, stop=True),
                 reads=[T_wuvp, T_ol], writes=[T_bg])
            r0 = (h % 2) * 64
            kt_o = 4 + h // 2
            if evac_eng():
                S.op(S.act, lambda: A.copy(mixT[r0:r0 + 64, kt_o, b0:b1], bg[r0:r0 + 64, b0:b1]), reads=[T_bg], writes=[T_mixT])
            else:
                S.op(S.dve, lambda: V.tensor_copy(mixT[r0:r0 + 64, kt_o, b0:b1], bg[r0:r0 + 64, b0:b1]), reads=[T_bg], writes=[T_mixT])
        fin_box["fin"] = fin

    hview = PHA.rearrange("p (f n) -> p f n", f=NFB)
    qlat_all = hview[:, 0:8, 256:512]
    qrz_all = hview[:, 12:20, 256:512]
    T_qall = Tk("qall")

    def q_all_sample(l, td, wq, T_wq):
        NT = td.NT
        for h in range(8):
            i = h % 2
            bn, T_bn = gbank()
            for kt in range(2):
                S.op(S.pe, lambda kt=kt: P.matmul(bn[:64, :NT], wq[:, kt, h * 64:(h + 1) * 64], cqn[:, kt, :NT],
                                                  start=(kt == 0), stop=(kt == 1)),
                     reads=[T_PHB, T_wq], writes=[T_bn], sig=(kt == 1))
            S.op(S.act, lambda: A.copy(qn_h[i][:, :NT], bn[:64, :NT]), reads=[T_bn], writes=[T_qn[i]])
            bl, T_bl = gbank()
            S.op(S.pe, lambda: P.matmul(bl[:, :NT], wukT[:, l, h, :], qn_h[i][:, :NT], start=True, stop=True),
                 reads=[T_qn[i], T_wukT], writes=[T_bl])
            S.op(S.dve, lambda: V.tensor_copy(qlat_all[:, h, :NT], bl[:, :NT]), reads=[T_bl], writes=[T_qall])
            g, hh = h // 4, h % 4
            S.op(S.act, lambda: A.activation(out=qrz_all[:, h, :NT], in_=qr4[g][:, :NT], func=AF.Copy, scale=hmask[:, hh:hh + 1]),
                 reads=[T_qr4[g], T_cst], writes=[T_qall])

    def attn_sample_seg(l, td, s, keytiles):
        DSq = td.PS
        q0 = s * DSq
        nk = len(keytiles)
        NQ = 8 * DSq

        def scores(ki):
            kt, nkeys, c0, dj = keytiles[ki]
            bs, T_bs = abank_s()
            col = kt * 128
            bsv = bs[:nkeys, 0:NQ].rearrange("p (h c) -> p h c", h=8)
            S.op(S.pe, lambda: P.matmul(bsv, ckvT[l][:, col:col + nkeys], qlat_all[:, :, q0:q0 + DSq], start=True, stop=False),
                 reads=[T_cache[l], T_qall], writes=[T_bs], sig=False)
            S.op(S.pe, lambda: P.matmul(bsv, kr4T[l][:, col:col + nkeys], qrz_all[:, :, q0:q0 + DSq], start=False, stop=True),
                 reads=[T_cache[l], T_qall], writes=[T_bs])
            idx = gstate["pt"] % NPT
            gstate["pt"] += 1
            pt, T_pt = PTf[idx], T_PTf[idx]
            S.op(S.act, lambda: A.activation(out=pt[:nkeys, 0:NQ], in_=bs[:nkeys, 0:NQ], func=AF.Exp, scale=SCALE),
                 reads=[T_bs], writes=[T_pt])
            return pt, T_pt

        def pv(ki, pt, T_pt):
            kt, nkeys, c0, dj = keytiles[ki]
            S.op(S.pe, lambda: P.matmul(bankO[:, 0:NQ], ckvtok[l][:nkeys, kt, :], pt[:nkeys, 0:NQ], start=(ki == 0),
                                        stop=(ki == nk - 1), skip_group_check=True),
                 reads=[T_cache[l], T_pt], writes=[T_bankO], sig=False)
            S.op(S.pe, lambda: P.matmul(bankZ[:, 0:NQ], onesb[:nkeys, :], pt[:nkeys, 0:NQ], start=(ki == 0),
                                        stop=(ki == nk - 1), skip_group_check=True),
                 reads=[T_cstb, T_pt], writes=[T_bankZ], sig=(ki == nk - 1))

        pend = None
        for ki in range(nk):
            cur = scores(ki)
            if pend is not None:
                pv(ki - 1, *pend)
            pend = cur
            yield
        pv(nk - 1, *pend)
        S.op(S.act, lambda: A.activation(out=rz[:, 0:NQ], in_=bankZ[:, 0:NQ], func=AF.Ln), reads=[T_bankZ], writes=[T_rz])
        S.op(S.act, lambda: A.activation(out=rz[:, 0:NQ], in_=rz[:, 0:NQ], func=AF.Exp, scale=-1.0), reads=[T_rz], writes=[T_rz])
        ol, T_ol = olat2[s % 2], T_olat2[s % 2]
        S.op(S.dve, lambda: V.tensor_tensor(out=ol[:, 0:NQ], in0=bankO[:, 0:NQ], in1=rz[:, 0:NQ], op=ALU.mult),
             reads=[T_bankO, T_rz], writes=[T_ol])
        for h in range(8):
            bg, T_bg = gbank()
            S.op(S.pe, lambda h=h, bg=bg: P.matmul(bg[:, 0:DSq], wuvp[:, l, h, :], ol[:, h * DSq:(h + 1) * DSq], start=True, stop=True),
                 reads=[T_wuvp, T_ol], writes=[T_bg])
            r0 = (h % 2) * 64
            kt_o = 4 + h // 2
            if evac_eng():
                S.op(S.act, lambda bg=bg: A.copy(mixT[r0:r0 + 64, kt_o, q0:q0 + DSq], bg[r0:r0 + 64, 0:DSq]), reads=[T_bg], writes=[T_mixT])
            else:
                S.op(S.dve, lambda bg=bg: V.tensor_copy(mixT[r0:r0 + 64, kt_o, q0:q0 + DSq], bg[r0:r0 + 64, 0:DSq]), reads=[T_bg], writes=[T_mixT])

    def mix_out(l, td):
        PS, Sn, NT = td.PS, td.S, td.NT
        for kt in range(4):
            bk, T_bk = gbank()
            reg = bk.bitcast(BF16)
            for s in range(Sn):
                S.op(S.pe, lambda s=s, kt=kt, reg=reg: P.transpose(reg[:, s * PS:(s + 1) * PS],
                                                                   mixtok[:PS, s, kt * 128:(kt + 1) * 128], identb[:PS, :PS]),
                     reads=[T_mixtok, T_cstb], writes=[T_bk], sig=(s == Sn - 1))
            S.op(S.dve, lambda kt=kt, reg=reg: V.tensor_scalar(out=mixT[:, kt, :NT], in0=reg[:, :NT], scalar1=g_ssd(l)[:, kt:kt + 1],
                                                               scalar2=None, op0=ALU.mult),
                 reads=[T_bk, T_prm], writes=[T_mixT])
        for hf in range(2):
            wo, T_wo = ws_get(l, C_O0 + hf)
            wo = wo.rearrange("p (k n) -> p k n", k=8)
            for s in range(Sn):
                bk, T_bk = gbank()
                for kt in range(8):
                    S.op(S.pe, lambda kt=kt, s=s, bk=bk, wo=wo: P.matmul(bk[:PS, :], mixT[:, kt, s * PS:(s + 1) * PS], wo[:, kt, :],
                                                                          start=(kt == 0), stop=(kt == 7)),
                         reads=[T_mixT, T_wo], writes=[T_bk], sig=(kt == 7))
                S.op(S.dve, lambda s=s, bk=bk, hf=hf: V.tensor_tensor(out=xt[:PS, s, hf * 512:(hf + 1) * 512],
                                                                      in0=xt[:PS, s, hf * 512:(hf + 1) * 512], in1=bk[:PS, :], op=ALU.add),
                     reads=[T_bk, T_xts[s]], writes=[T_xts[s]])

    def ffn(l, td, on_ready, on_done):
        PS, Sn, NT = td.PS, td.S, td.NT
        first = True
        for i in range(11):
            wg_, T_wg = ws_get(l, C_GU + i)
            wg_ = wg_.rearrange("p (j g k n) -> p j g k n", j=2, g=2, k=8)
            for j in range(2):
                fb = 2 * i + j
                bg, T_bg = gbank()
                bu, T_bu = gbank()
                for kt in range(8):
                    S.op(S.pe, lambda kt=kt, bg=bg, j=j, wg_=wg_: P.matmul(bg[:, :NT], wg_[:, j, 0, kt, :], hT[:, kt, :NT],
                                                                            start=(kt == 0), stop=(kt == 7)),
                         reads=[T_hT, T_wg], writes=[T_bg], sig=(kt == 7))
                for kt in range(8):
                    S.op(S.pe, lambda kt=kt, bu=bu, j=j, wg_=wg_: P.matmul(bu[:, :NT], wg_[:, j, 1, kt, :], hT[:, kt, :NT],
                                                                            start=(kt == 0), stop=(kt == 7)),
                         reads=[T_hT, T_wg], writes=[T_bu], sig=(kt == 7))
                sgi, T_sgi = sg[fb % 2], T_sg[fb % 2]
                S.op(S.act, lambda bg=bg, sgi=sgi: A.activation(out=sgi[:, :NT], in_=bg[:, :NT], func=AF.Silu),
                     reads=[T_bg], writes=[T_sgi])
                wl = [T_hmid]
                if first:
                    wl = [T_hmid, T_convout, T_mixtok, T_mixT]
                    first = False
                S.op(S.dve, lambda bu=bu, sgi=sgi, fb=fb: V.tensor_tensor(out=hmid[:, fb, :NT], in0=sgi[:, :NT], in1=bu[:, :NT],
                                                                          op=ALU.mult),
                     reads=[T_bu, T_sgi], writes=wl)
        accs = [gbank() for s in range(Sn)]
        for c in range(3):
            wd_, T_wd = ws_get(l, C_D + c)
            wd_ = wd_.rearrange("p (k n) -> p k n", k=8)
            nk = 8 if c < 2 else NFB - 16
            for k in range(nk):
                fb = c * 8 + k
                for s in range(Sn):
                    bk, T_bk = accs[s]
                    S.op(S.pe, lambda k=k, s=s, bk=bk, fb=fb, wd_=wd_: P.matmul(bk[:PS, :], hmid[:, fb, s * PS:(s + 1) * PS],
                                                                                 wd_[:, k, :], start=(fb == 0), stop=(fb == NFB - 1)),
                         reads=[T_hmid, T_wd], writes=[T_bk], sig=(fb == NFB - 1))
        for s in range(Sn):
            bk, T_bk = accs[s]
            S.op(S.dve, lambda s=s, bk=bk: V.tensor_tensor(out=xt[:PS, s, 0:512], in0=xt[:PS, s, 0:512], in1=bk[:PS, :], op=ALU.add),
                 reads=[T_bk, T_xts[s]], writes=[T_xts[s]])
        wds = []
        for c in range(3):
            wd_, T_wd = ws_get(l, C_D + 3 + c, live_prev=c)
            wds.append((wd_.rearrange("p (k n) -> p k n", k=8), T_wd))
        for s in range(Sn):
            bk, T_bk = banks[4 + s], T_bank[4 + s]
            for fb in range(NFB):
                wd_, T_wd = wds[fb // 8]
                S.op(S.pe, lambda s=s, bk=bk, fb=fb, wd_=wd_: P.matmul(bk[:PS, :], hmid[:, fb, s * PS:(s + 1) * PS],
                                                                        wd_[:, fb % 8, :], start=(fb == 0), stop=(fb == NFB - 1)),
                     reads=[T_hmid, T_wd], writes=[T_bk], sig=(fb == NFB - 1))
            S.op(S.dve, lambda s=s, bk=bk: V.tensor_tensor(out=xt[:PS, s, 512:1024], in0=xt[:PS, s, 512:1024], in1=bk[:PS, :], op=ALU.add),
                 reads=[T_bk, T_xts[s]], writes=[T_xts[s]])
            on_ready(s)
        on_done()

    def final_norm_store_sub(td, s):
        PS = td.PS
        st, T_st = nst[s], T_nst[s]
        S.op(S.act, lambda: A.activation(out=junk[:PS, :], in_=xt[:PS, s, :], func=AF.Square, accum_out=st[:PS, 0:1]),
             reads=[T_xts[s]], writes=[T_junk, T_st])
        S.op(S.act, lambda: A.activation(out=st[:PS, 1:2], in_=st[:PS, 0:1], func=AF.Ln, scale=1.0 / D_MODEL, bias=epsb[:PS, 0:1]),
             reads=[T_st, T_eps], writes=[T_st])
        S.op(S.act, lambda: A.activation(out=st[:PS, 2:3], in_=st[:PS, 1:2], func=AF.Exp, scale=-0.5), reads=[T_st], writes=[T_st])
        S.op(S.dve, lambda: V.scalar_tensor_tensor(out=xt[:PS, s, :], in0=xt[:PS, s, :], scalar=st[:PS, 2:3], in1=gfin[:PS, :],
                                                   op0=ALU.mult, op1=ALU.mult),
             reads=[T_xts[s], T_st, T_gfin], writes=[T_xts[s]])
        if td.kind == "p":
            r0 = td.t * 512 + s * 128
            S.dma(S.pool, y_prompt[td.b, r0:r0 + 128, :], xt[:, s, :], reads=[T_xts[s]])
        else:
            S.dma(S.pool, y_sample[s], xt[:PS, s, :], reads=[T_xts[s]])

    def load_x_sub(td, s):
        if td.kind == "p":
            r0 = td.t * 512 + s * 128
            S.dma(S.sp, xt[:, s, :], x_prompt[td.b, r0:r0 + 128, :], writes=[T_xts[s]])
        else:
            S.dma(S.sp, xt[:td.PS, s, :], x_sample[s], writes=[T_xts[s]])

    hT_ready = False
    for ti, td in enumerate(tiles):
        PS, Sn, NT = td.PS, td.S, td.NT
        if ti == 0:
            for s in range(Sn):
                load_x_sub(td, s)
        if td.kind == "p":
            r0 = td.t * 512
            S.dma(S.sp, ropefm, rope_fm_d[:, :, r0:r0 + 512], writes=[T_ropefm])
            S.dma(S.sp, ropetm[:, 0:4, :], rope_tm_d[r0:r0 + 512, :].rearrange("(s p) c -> p s c", p=128), writes=[T_ropetm])
        else:
            for s in range(Sn):
                S.dma(S.sp, ropefm[:, :, s * DS:(s + 1) * DS], rope_fm_d[:, :, SEQ:SEQ + DS], writes=[T_ropefm])
                S.dma(S.sp, ropetm[:PS, s, :], rope_tm_d[SEQ:SEQ + DS, :], writes=[T_ropetm])
        for l in range(DEPTH):
            if td.kind == "p":
                if td.t == 0:
                    state_zero(l)
                    S.op(S.pool, lambda l=l: G.memset(convhist[l], 0.0), writes=[T_convhist[l]])
            if not hT_ready:
                rms_to_hT(td, g_mix(l))
            hT_ready = False
            in_proj_cq(l, td)
            cs = {"first": True, "pend": None}
            wx, T_wx = ws_get(l, C_X0)
            wx = wx.rearrange("p (k n) -> p k n", k=8)
            wk, T_wk = ws_get(l, C_DTKV, live_prev=1)
            wk = wk.rearrange("p (k n) -> p k n", k=8)
            for i in range(4):
                xbc_block(l, td, i, wx, T_wx, cs)
                if i < Sn:
                    proj_dtkv(l, td, i, wk, T_wk)
            small_token(l, td, 0)
            wx, T_wx = ws_get(l, C_X1)
            wx = wx.rearrange("p (k n) -> p k n", k=8)
            wz, T_wz = ws_get(l, C_Z, live_prev=1)
            wz = wz.rearrange("p (k n) -> p k n", k=8)
            for i in range(4):
                xbc_block(l, td, 4 + i, wx, T_wx, cs)
                if i < Sn:
                    proj_z(l, td, i, wz, T_wz)
            cs["pend"]()
            if td.kind == "p":
                for s in range(Sn):
                    kv_new(l, td, s, td.t * 4 + s, 128)
            wq, T_wq = ws_get(l, C_UQ)
            wq = wq[:, 0:2048].rearrange("p (k n) -> p k n", k=2)
            q_rope(l, td, wq, T_wq)
            if td.kind == "p":
                kts = []
                for kt in range(td.t * 4 + 4):
                    j = kt - td.t * 4
                    if j < 0:
                        kts.append((kt, 128, 0, None))
                    else:
                        kts.append((kt, 128, 128 * j, j))

                def attn_all(l=l, td=td, kts=kts, wq=wq, T_wq=T_wq):
                    fin_box = {"fin": None}
                    q_head(l, td, 0, wq, T_wq)
                    for h in range(8):
                        if h + 1 < 8:
                            q_head(l, td, h + 1, wq, T_wq)
                        yield from attn_head(l, td, h, 0, 512, kts, fin_box)
                    fin_box["fin"]()

                def ssd_all(l=l, td=td):
                    gens = [ssd_sub(l, td, s) for s in range(td.S)]
                    OFF = 8
                    for k in range(OFF * (td.S - 1) + 8):
                        for s in reversed(range(td.S)):
                            if 0 <= k - OFF * s < 8:
                                next(gens[s])
                        yield

                ga, gs = attn_all(), ssd_all()
                ratio = max(1, (8 * len(kts)) // (8 * Sn))
                done_a = done_s = False
                while not (done_a and done_s):
                    if not done_s:
                        try:
                            next(gs)
                        except StopIteration:
                            done_s = True
                    for _ in range(ratio):
                        if not done_a:
                            try:
                                next(ga)
                            except StopIteration:
                                done_a = True
                if td.t == NTP - 1:
                    state_store(l, o_ssm_p[l, td.b])
                    for k in range(3):
                        S.dma(S.pool, o_conv_p[l, td.b, k].rearrange("(b p) -> p b", p=128), convhist[l][:, :, 0, k],
                              reads=[T_convhist[l]])
            else:
                def ssd_sample(l=l, td=td):
                    for s in range(td.S):
                        state_load(l, s)
                        yield
                        yield from ssd_sub(l, td, s)
                        state_store(l, o_ssm_s[l, s])
                        yield
                def attn_sample(l=l, td=td, wq=wq, T_wq=T_wq):
                    q_all_sample(l, td, wq, T_wq)
                    yield
                    for s in range(td.S):
                        kv_past(l, s)
                        kv_new(l, td, s, NPK, DS)
                        kts = [(kt, 128, 0, None) for kt in range(NPK)] + [(NPK, DS, 0, None)]
                        yield from attn_sample_seg(l, td, s, kts)
                        yield

                ga, gs = attn_sample(), ssd_sample()
                done_a = done_s = False
                while not (done_a and done_s):
                    if not done_s:
                        try:
                            next(gs)
                        except StopIteration:
                            done_s = True
                    if not done_a:
                        try:
                            next(ga)
                        except StopIteration:
                            done_a = True
                for s in range(Sn):
                    for k in range(3):
                        S.dma(S.pool, o_conv_s[l, s, k].rearrange("(b p) -> p b", p=128), convhist_s[l][:, :, s, k],
                              reads=[T_convhist_s[l]])
            if td is tiles[0] and l == 0:
                for l2 in range(1, DEPTH):
                    cast_weights(l2)
            mix_out(l, td)
            rms_to_hT(td, g_ffn(l))
            if l + 1 < DEPTH:
                def on_ready(s, td=td, l=l):
                    norm_stats(td, s)
                    if s >= 1:
                        norm_tr(td, s - 1, g_mix(l + 1))

                def on_done(td=td, l=l):
                    norm_tr(td, td.S - 1, g_mix(l + 1))
                ffn(l, td, on_ready, on_done)
                hT_ready = True
            else:
                nxt = tiles[ti + 1] if ti + 1 < len(tiles) else None

                def on_ready(s, td=td, nxt=nxt):
                    final_norm_store_sub(td, s)
                    if nxt is not None and s < nxt.S:
                        load_x_sub(nxt, s)
                ffn(l, td, on_ready, lambda: None)

    S.finish()
    nci.__exit__(None, None, None)
    stats = {e.name: (e.nins, e.nwait) for e in (S.pe, S.act, S.dve, S.pool, S.sp)}
    return nc, stats


def host_consts(cfg):
    c = np.zeros((128, 5 * 128), np.float32)
    j = np.arange(128)
    c[:, 0:128] = np.eye(128, dtype=np.float32)
    c[:, 128:256] = (j[:, None] <= j[None, :]).astype(np.float32)
    c[:, 256:384] = (j[:, None] > j[None, :]).astype(np.float32)
    c[:, 384:512] = 1.0
    for hh in range(4):
        c[hh * 32:(hh + 1) * 32, 512 + hh] = 1.0
    pos = np.concatenate([np.arange(cfg.SEQ), cfg.PAST + np.arange(cfg.DS)]).astype(np.float32)
    inv = (1.0 / (ROPE_BASE ** (np.arange(0, QK_ROPE, 2, dtype=np.float32) / QK_ROPE))).astype(np.float32)
    ang = pos[:, None] * inv[None, :]
    cos, sin = np.cos(ang).astype(np.float32), np.sin(ang).astype(np.float32)
    tm = np.concatenate([cos, sin], axis=1).astype(np.float32)
    fm = np.zeros((128, 2, pos.shape[0]), np.float32)
    for hh in range(4):
        fm[hh * 32:hh * 32 + 16, 0] = cos.T
        fm[hh * 32 + 16:hh * 32 + 32, 0] = cos.T
        fm[hh * 32:hh * 32 + 16, 1] = -sin.T
        fm[hh * 32 + 16:hh * 32 + 32, 1] = sin.T
    return c, fm, tm


_CACHE = {}


def run(cfg, inputs, ncores):
    key = (cfg.PB, cfg.SEQ, cfg.SB, cfg.PAST, cfg.DEPTH)
    if key not in _CACHE:
        _CACHE[key] = build(cfg)
    nc, stats = _CACHE[key]
    c, fm, tm = host_consts(cfg)
    PB, SB = cfg.PB, cfg.SB
    f = lambda a: np.ascontiguousarray(np.asarray(a, dtype=np.float32))
    in_maps = []
    for i in range(ncores):
        m = {}
        m["x_prompt"] = f(inputs["x_prompt"][i * PB:(i + 1) * PB])
        m["x_sample"] = f(inputs["x_sample"][i * SB:(i + 1) * SB])
        m["cache_mla_ckv"] = f(inputs["cache_mla_ckv"][:, i * SB:(i + 1) * SB])
        m["cache_mla_krope"] = f(inputs["cache_mla_krope"][:, i * SB:(i + 1) * SB])
        m["state_ssm"] = f(inputs["state_ssm"][:, i * SB:(i + 1) * SB])
        m["state_conv"] = f(inputs["state_conv"][:, i * SB:(i + 1) * SB])
        for k in ("w_in", "w_uq", "w_uk", "w_uv", "w_out", "norm_mix", "q_norm", "kv_norm", "ssd_norm", "conv_w", "conv_b",
                  "dt_bias", "a_log", "d_skip", "norm_ffn", "w_gate", "w_up", "w_down", "norm_final"):
            m[k] = f(inputs[k])
        m["consts"] = c
        m["rope_fm"] = fm
        m["rope_tm"] = tm
        in_maps.append(m)
    res = run_bass_kernel_spmd(nc, in_maps, core_ids=list(range(ncores)))
    R = res.results
    cat0 = lambda k: np.concatenate([np.asarray(r[k], dtype=np.float32) for r in R], axis=0)
    cat1 = lambda k: np.concatenate([np.asarray(r[k], dtype=np.float32) for r in R], axis=1)
    return (cat0("y_prompt"), cat0("y_sample"), cat1("o_ckv_p"), cat1("o_kr_p"), cat1("o_ssm_p"), cat1("o_conv_p"),
            cat1("o_ckv_s"), cat1("o_kr_s"), cat1("o_ssm_s"), cat1("o_conv_s"))


def kernel(**inputs):
    cfg = Cfg(PB=4, SEQ=2048, SB=4, PAST=1024, DEPTH=2)
    return run(cfg, inputs, 8)
```

```python
import numpy as np
import concourse.bass as bass
import concourse.mybir as mybir
from concourse.bass_utils import run_bass_kernel_spmd

F32 = mybir.dt.float32
BF16 = mybir.dt.bfloat16
AF = mybir.ActivationFunctionType
ALU = mybir.AluOpType

D_MODEL = 1024
EPS = 1e-6
SSD_HEADS = 8
D_STATE = 128
CONV_W = 4
MLA_HEADS = 8
QK_NOPE = 64
QK_ROPE = 32
Q_LORA = 256
KV_LORA = 128
S_Z = 512
S_XBC = 1536
S_DT = 1544
S_Q = 1800
IN_PROJ = 1960
D_FF = 2816
NFB = D_FF // 128
ROPE_BASE = 10000.0
SCALE = (QK_NOPE + QK_ROPE) ** -0.5

C_Z, C_DTKV, C_X0, C_X1, C_CQ, C_UQ, C_O0, C_O1 = range(8)
C_GU = 8
C_D = 19
NCHUNK = 25


class Cfg:
    def __init__(self, PB=4, SEQ=2048, SB=4, PAST=1024, DEPTH=2):
        self.PB, self.SEQ, self.SB, self.PAST, self.DEPTH = PB, SEQ, SB, PAST, DEPTH
        self.DS = 64
        self.NTP = SEQ // 512


class Tk:
    __slots__ = ("w", "r", "name", "busy", "hold", "stamp")

    def __init__(self, name=""):
        self.w = None
        self.r = {}
        self.name = name
        self.busy = False
        self.hold = False
        self.stamp = 0


class Eng:
    def __init__(self, S, name, e, is_pe=False):
        self.S = S
        self.name = name
        self.e = e
        self.is_pe = is_pe
        self.sem = S.nc.alloc_semaphore("sem_" + name)
        self.key = ("E", name)
        S.sems[self.key] = self.sem
        self.cnt = 0
        self.seen = {}
        self.nwait = 0
        self.nins = 0


class Sched:
    NDMA = 8

    def __init__(self, nc):
        self.nc = nc
        self.sems = {}
        self.stampc = 0
        self.pe = Eng(self, "pe", nc.tensor, True)
        self.act = Eng(self, "act", nc.scalar)
        self.dve = Eng(self, "dve", nc.vector)
        self.pool = Eng(self, "pool", nc.gpsimd)
        self.sp = Eng(self, "sp", nc.sync)
        self.dq = {}
        for q in (self.sp, self.pool):
            sl = []
            for i in range(self.NDMA):
                key = ("D", q.name, i)
                self.sems[key] = nc.alloc_semaphore("dsem_%s_%d" % (q.name, i))
                sl.append(key)
            self.dq[q.name] = [sl, 0]

    def _wait(self, eng, deps):
        for key, val in deps:
            if eng.is_pe and key == eng.key:
                continue
            if eng.seen.get(key, 0) >= val:
                continue
            eng.e.wait_ge(self.sems[key], val)
            eng.seen[key] = val
            eng.nwait += 1

    def _deps(self, reads, writes):
        deps = {}
        for t in reads:
            if t.w is not None:
                k, v = t.w
                if deps.get(k, 0) < v:
                    deps[k] = v
        for t in writes:
            if t.w is not None:
                k, v = t.w
                if deps.get(k, 0) < v:
                    deps[k] = v
            for k, v in t.r.items():
                if deps.get(k, 0) < v:
                    deps[k] = v
        return list(deps.items())

    def _mark(self, ticket, reads, writes):
        k, v = ticket
        for t in reads:
            if t.r.get(k, 0) < v:
                t.r[k] = v
            if t.busy and not t.hold:
                t.busy = False
                self.stampc += 1
                t.stamp = self.stampc
        for t in writes:
            t.w = ticket
            t.r = {}

    def op(self, eng, fn, reads=(), writes=(), sig=True):
        self._wait(eng, self._deps(reads, writes))
        ins = fn()
        eng.nins += 1
        if sig:
            eng.cnt += 1
            ins.then_inc(eng.sem, 1)
            ticket = (eng.key, eng.cnt)
        else:
            assert eng.is_pe
            ticket = (eng.key, eng.cnt + 1)
        self._mark(ticket, reads, writes)
        return ins

    def dma(self, q, out, in_, reads=(), writes=(), **kw):
        self._wait(q, self._deps(reads, writes))
        sl, k = self.dq[q.name]
        key = sl[k % self.NDMA]
        rnd = k // self.NDMA
        if rnd > 0:
            self._wait(q, [(key, 16 * rnd)])
        ins = q.e.dma_start(out=out, in_=in_, **kw)
        ins.then_inc(self.sems[key], 16)
        ticket = (key, 16 * (rnd + 1))
        self.dq[q.name][1] = k + 1
        q.nins += 1
        self._mark(ticket, reads, writes)
        return ticket

    def finish(self):
        for qn, (sl, k) in self.dq.items():
            q = {"sp": self.sp, "pool": self.pool}[qn]
            for i in range(min(k, self.NDMA)):
                last = k - 1 - i
                key = sl[last % self.NDMA]
                self._wait(q, [(key, 16 * (last // self.NDMA + 1))])


class TileD:
    def __init__(self, cfg, kind, b=0, t=0):
        self.kind = kind
        self.b = b
        self.t = t
        if kind == "p":
            self.S, self.PS = 4, 128
            self.NSEG, self.LSEG = 1, 512
        else:
            self.S, self.PS = cfg.SB, 64
            self.NSEG, self.LSEG = cfg.SB, 64
        self.NT = self.S * self.PS


def build(cfg):
    nc = bass.Bass("TRN2", target_bir_lowering=False)
    S = Sched(nc)
    P, A, V, G = nc.tensor, nc.scalar, nc.vector, nc.gpsimd
    PB, SEQ, SB, PAST, DEPTH, DS, NTP = cfg.PB, cfg.SEQ, cfg.SB, cfg.PAST, cfg.DEPTH, cfg.DS, cfg.NTP
    NPK = PAST // 128
    NKT = max(SEQ // 128, NPK + 1)
    KCAP = NKT * 128

    def din(name, shape, dt=F32):
        return nc.dram_tensor(name, list(shape), dt, kind="ExternalInput").ap()

    def dout(name, shape, dt=F32):
        return nc.dram_tensor(name, list(shape), dt, kind="ExternalOutput").ap()

    x_prompt = din("x_prompt", [PB, SEQ, D_MODEL])
    x_sample = din("x_sample", [SB, DS, D_MODEL])
    cache_ckv = din("cache_mla_ckv", [DEPTH, SB, PAST, KV_LORA])
    cache_kr = din("cache_mla_krope", [DEPTH, SB, PAST, QK_ROPE])
    state_ssm = din("state_ssm", [DEPTH, SB, SSD_HEADS, 64, D_STATE])
    state_conv = din("state_conv", [DEPTH, SB, 3, 1024])
    w_in = din("w_in", [DEPTH, D_MODEL, IN_PROJ])
    w_uq = din("w_uq", [DEPTH, Q_LORA, 768])
    w_uk = din("w_uk", [DEPTH, KV_LORA, 8, 64])
    w_uv = din("w_uv", [DEPTH, KV_LORA, 8, 64])
    w_out = din("w_out", [DEPTH, 1024, D_MODEL])
    norm_mix = din("norm_mix", [DEPTH, D_MODEL])
    q_norm = din("q_norm", [DEPTH, Q_LORA])
    kv_norm = din("kv_norm", [DEPTH, KV_LORA])
    ssd_norm = din("ssd_norm", [DEPTH, 512])
    conv_w = din("conv_w", [DEPTH, CONV_W, 1024])
    conv_b = din("conv_b", [DEPTH, 1024])
    dt_bias = din("dt_bias", [DEPTH, 8])
    a_log = din("a_log", [DEPTH, 8])
    d_skip = din("d_skip", [DEPTH, 8])
    norm_ffn = din("norm_ffn", [DEPTH, D_MODEL])
    w_gate = din("w_gate", [DEPTH, D_MODEL, D_FF])
    w_up = din("w_up", [DEPTH, D_MODEL, D_FF])
    w_down = din("w_down", [DEPTH, D_FF, D_MODEL])
    norm_final = din("norm_final", [D_MODEL])
    consts_d = din("consts", [128, 5 * 128])
    rope_fm_d = din("rope_fm", [128, 2, SEQ + DS])
    rope_tm_d = din("rope_tm", [SEQ + DS, 32])

    y_prompt = dout("y_prompt", [PB, SEQ, D_MODEL])
    y_sample = dout("y_sample", [SB, DS, D_MODEL])
    o_ckv_p = dout("o_ckv_p", [DEPTH, PB, SEQ, KV_LORA])
    o_kr_p = dout("o_kr_p", [DEPTH, PB, SEQ, QK_ROPE])
    o_ssm_p = dout("o_ssm_p", [DEPTH, PB, 8, 64, D_STATE])
    o_conv_p = dout("o_conv_p", [DEPTH, PB, 3, 1024])
    o_ckv_s = dout("o_ckv_s", [DEPTH, SB, DS, KV_LORA])
    o_kr_s = dout("o_kr_s", [DEPTH, SB, DS, QK_ROPE])
    o_ssm_s = dout("o_ssm_s", [DEPTH, SB, 8, 64, D_STATE])
    o_conv_s = dout("o_conv_s", [DEPTH, SB, 3, 1024])

    scr = nc.dram_tensor("wscratch", [DEPTH, NCHUNK, 128, 4096], BF16, kind="Internal").ap()
    T_scr = [[Tk("scr%d_%d" % (l, c)) for c in range(NCHUNK)] for l in range(DEPTH)]

    def sb(name, shape, dt=F32):
        return nc.alloc_sbuf_tensor(name, list(shape), dt).ap()

    cst = sb("cst", [128, 5 * 128]); T_cst = Tk("cst")
    identf = cst[:, 0:128]
    Uf = cst[:, 128:256]
    Mstf = cst[:, 256:384]
    onesf = cst[:, 384:512]
    hmask = cst[:, 512:516]
    cstb = sb("cstb", [128, 3 * 128], BF16); T_cstb = Tk("cstb")
    identb = cstb[:, 0:128]
    onesb = cstb[:, 128:256]
    epsb = sb("epsb", [128, 2]); T_eps = Tk("eps")

    NSLOT = 4
    wring = [sb("wring%d" % i, [128, 4096], BF16) for i in range(NSLOT)]
    T_wring = [Tk("wring%d" % i) for i in range(NSLOT)]

    xt = sb("xt", [128, 4, 1024]); T_xts = [Tk("xt%d" % i) for i in range(4)]
    nst = [sb("nst%d" % i, [128, 4]) for i in range(4)]; T_nst = [Tk("nst%d" % i) for i in range(4)]
    PHB = sb("PHB", [128, 4096], BF16)
    xn = PHB.rearrange("p (s d) -> p s d", s=4)
    cq2 = PHB[:, 0:1024].rearrange("p (k n) -> p k n", k=2)
    cqn = PHB[:, 1024:2048].rearrange("p (k n) -> p k n", k=2)
    rstdb = PHB[:, 2048:3072].bitcast(F32)
    sqtmp = PHB[:, 3072:4096].bitcast(F32)
    T_PHB = Tk("PHB")
    hT = sb("hT", [128, 8, 512], BF16); T_hT = Tk("hT")
    PHA = sb("PHA", [128, NFB * 512], BF16)
    hmid = PHA.rearrange("p (f n) -> p f n", f=NFB)
    convout = PHA[:, 0:4096].rearrange("p (k n) -> p k n", k=8)
    mixtok = PHA[:, 4096:6144].rearrange("p (s d) -> p s d", s=4)
    mixT = PHA[:, 6144:10240].rearrange("p (k n) -> p k n", k=8)
    T_hmid = Tk("hmid"); T_convout = Tk("convout"); T_mixtok = Tk("mixtok"); T_mixT = Tk("mixT")
    junk = sb("junk", [128, 1024], BF16); T_junk = Tk("junk")
    nstat = sb("nstat", [128, 16]); T_nstat = Tk("nstat")
    NCB = 2
    utmp = [sb("utmp%d" % i, [128, 516]) for i in range(NCB)]
    T_utmp = [Tk("utmp%d" % i) for i in range(NCB)]
    cacc = [sb("cacc%d" % i, [128, 512]) for i in range(NCB)]
    T_cacc = [Tk("cacc%d" % i) for i in range(NCB)]
    zs = sb("zs", [128, 4, 512], BF16); T_zs = Tk("zs")
    tk = sb("tk", [128, 4, 168]); T_tk = Tk("tk")
    ckvn = sb("ckvn", [128, 4, 128]); T_ckvn = Tk("ckvn")
    krn = sb("krn", [128, 4, 32]); T_krn = Tk("krn")
    ckvb = sb("ckvb", [128, 4, 128], BF16); T_ckvb = Tk("ckvb")
    kr4b = sb("kr4b", [128, 4, 128], BF16); T_kr4b = Tk("kr4b")
    dtt = sb("dtt", [128, 4, 8]); T_dtt = Tk("dtt")
    att = sb("att", [128, 4, 8]); T_att = Tk("att")
    dtmp = sb("dtmp", [128, 6, 64]); T_dtmp = Tk("dtmp")
    ropetm = sb("ropetm", [128, 4, 32]); T_ropetm = Tk("ropetm")
    ropefm = sb("ropefm", [128, 2, 512]); T_ropefm = Tk("ropefm")
    XBt = [sb("XBt%d" % i, [128, 768], BF16) for i in range(2)]
    T_XBt = [Tk("XBt0"), Tk("XBt1")]
    Rt = sb("Rt", [128, 8, 128]); T_Rt = Tk("Rt")
    Et = sb("Et", [128, 8, 128]); T_Et = Tk("Et")
    CBm = sb("CBm", [128, 2, 128]); T_CBm = Tk("CBm")
    Wb = sb("Wb", [128, 8, 128], BF16); T_Wb = Tk("Wb")
    xD = sb("xD", [128, 512], BF16); T_xD = Tk("xD")
    xw = sb("xw", [128, 512], BF16); T_xw = Tk("xw")
    ex2 = [sb("ex%d" % i, [128, 32]) for i in range(2)]; T_ex2 = [Tk("ex0"), Tk("ex1")]
    ytmp = sb("ytmp", [128, 512]); T_ytmp = Tk("ytmp")
    yg = sb("yg", [128, 512]); T_yg = Tk("yg")
    hTf = [sb("hTf%d" % l, [128, 512]) for l in range(DEPTH)]
    hTb = [sb("hTb%d" % l, [128, 512], BF16) for l in range(DEPTH)]
    T_hT_st = [Tk("hst%d" % l) for l in range(DEPTH)]
    sttmp = sb("sttmp", [128, 4, 128]); T_sttmp = Tk("sttmp")
    convhist = [sb("convhist%d" % l, [128, 8, 4, 3]) for l in range(DEPTH)]
    T_convhist = [Tk("ch%d" % l) for l in range(DEPTH)]
    convhist_s = [sb("convhist_s%d" % l, [128, 8, 4, 3]) for l in range(DEPTH)]
    T_convhist_s = [Tk("chs%d" % l) for l in range(DEPTH)]
    ckvT = [sb("ckvT%d" % l, [128, KCAP], BF16) for l in range(DEPTH)]
    kr4T = [sb("kr4T%d" % l, [128, KCAP], BF16) for l in range(DEPTH)]
    ckvtok = [sb("ckvtok%d" % l, [128, NKT, 128], BF16) for l in range(DEPTH)]
    T_cache = [Tk("cache%d" % l) for l in range(DEPTH)]
    qn_h = [sb("qn%d" % i, [64, 512], BF16) for i in range(2)]
    T_qn = [Tk("qn0"), Tk("qn1")]
    qlat_h = [sb("qlat%d" % i, [128, 512], BF16) for i in range(2)]
    T_qlat = [Tk("qlat0"), Tk("qlat1")]
    qr4 = [sb("qr4_%d" % i, [128, 512], BF16) for i in range(2)]
    T_qr4 = [Tk("qr40"), Tk("qr41")]
    qrz = [sb("qrz%d" % i, [128, 512], BF16) for i in range(2)]
    T_qrz = [Tk("qrz0"), Tk("qrz1")]
    ropet = [ytmp, yg]
    T_ropet = [T_ytmp, T_yg]
    NPT = 3
    PTf = [sb("PTf%d" % i, [128, 512], BF16) for i in range(NPT)]
    T_PTf = [Tk("PTf%d" % i) for i in range(NPT)]
    PTd = [sb("PTd%d" % i, [128, 512], BF16) for i in range(4)]
    T_PTd = [Tk("PTd%d" % i) for i in range(4)]
    rz = cacc[0]; T_rz = T_cacc[0]
    olat2 = [sb("olat%d" % i, [128, 512], BF16) for i in range(2)]; T_olat2 = [Tk("olat0"), Tk("olat1")]
    sg = [XBt[0][:, 0:512], XBt[1][:, 0:512]]
    T_sg = [T_XBt[0], T_XBt[1]]

    pastkr = sb("pastkr", [128, NPK, 32]); T_pastkr = Tk("pastkr")
    pastb = sb("pastb", [128, NPK, 128], BF16); T_pastb = Tk("pastb")
    pastkb = sb("pastkb", [128, NPK, 128], BF16); T_pastkb = Tk("pastkb")
    NPRM = DEPTH * (8 + 8 + 4 + 2 + 32 + 8 + 8 + 8 + 8 + 8) + 8
    prm = sb("prm", [128, DEPTH, 96]); T_prm = Tk("prm")
    g_mix = lambda l: prm[:, l, 0:8]
    g_ffn = lambda l: prm[:, l, 8:16]
    g_ssd = lambda l: prm[:, l, 16:20]
    g_q = lambda l: prm[:, l, 20:22]
    cw = lambda l: prm[:, l, 24:56].rearrange("p (b k) -> p b k", k=4)
    cb = lambda l: prm[:, l, 56:64]
    dtb = lambda l: prm[:, l, 64:72]
    A_b = lambda l: prm[:, l, 72:80]
    D_b = lambda l: prm[:, l, 80:88]
    gkv = sb("gkv", [128, DEPTH, 128]); T_gkv = Tk("gkv")
    gfin = sb("gfin", [128, 1024]); T_gfin = Tk("gfin")
    wukT = sb("wukT", [64, DEPTH, 8, 128], BF16); T_wukT = Tk("wukT")
    wuvp = sb("wuvp", [128, DEPTH, 8, 128], BF16); T_wuvp = Tk("wuvp")

    banks = [nc.alloc_psum_tensor("bank%d" % i, [128, 512], F32).ap() for i in range(8)]
    T_bank = [Tk("bank%d" % i) for i in range(8)]
    gstate = {"g": 0, "a": 0, "pt": 0}

    def gbank(hold=False):
        cands = [i for i in range(4) if not T_bank[i].busy]
        assert cands, "no free PSUM bank"
        i = min(cands, key=lambda i: T_bank[i].stamp)
        T_bank[i].busy = True
        T_bank[i].hold = hold
        return banks[i], T_bank[i]

    def unhold(T):
        T.hold = False

    def abank_s():
        i = 4 + gstate["a"] % 2
        gstate["a"] += 1
        return banks[i], T_bank[i]

    bankO, T_bankO = banks[6], T_bank[6]
    bankZ, T_bankZ = banks[7], T_bank[7]

    nci = nc.allow_non_contiguous_dma(reason="small parameter gathers")
    nci.__enter__()

    S.dma(S.sp, cst, consts_d, writes=[T_cst])
    if PB > 0:
        for s in range(4):
            S.dma(S.sp, xt[:, s, :], x_prompt[0, s * 128:(s + 1) * 128, :], writes=[T_xts[s]])
    S.op(S.dve, lambda: V.tensor_copy(cstb[:, 0:128], identf), reads=[T_cst], writes=[T_cstb])
    S.op(S.dve, lambda: V.tensor_copy(cstb[:, 128:256], onesf), reads=[T_cst], writes=[T_cstb])
    S.op(S.pool, lambda: G.memset(epsb[:, 0:1], EPS), writes=[T_eps])
    S.op(S.pool, lambda: G.memset(epsb[:, 1:2], 1.0), writes=[T_eps])
    for j in range(4):
        S.op(S.pool, lambda j=j: G.memset(PTd[j], 0.0), writes=[T_PTd[j]])
    S.op(S.pool, lambda: G.memset(wuvp, 0.0), writes=[T_wuvp])
    S.op(S.pool, lambda: G.memset(prm, 0.0), writes=[T_prm])
    for l in range(DEPTH):
        S.dma(S.sp, prm[:, l, 0:8], norm_mix[l].rearrange("(k p) -> p k", p=128), writes=[T_prm])
        S.dma(S.sp, prm[:, l, 8:16], norm_ffn[l].rearrange("(k p) -> p k", p=128), writes=[T_prm])
        S.dma(S.sp, prm[:, l, 16:20], ssd_norm[l].rearrange("(k p) -> p k", p=128), writes=[T_prm])
        S.dma(S.sp, prm[:, l, 20:22], q_norm[l].rearrange("(k p) -> p k", p=128), writes=[T_prm])
        for k in range(4):
            S.dma(S.sp, prm[:, l, 24:56].rearrange("p (b k) -> p b k", k=4)[:, :, k],
                  conv_w[l, k].rearrange("(b p) -> p b", p=128), writes=[T_prm])
        S.dma(S.sp, prm[:, l, 56:64], conv_b[l].rearrange("(k p) -> p k", p=128), writes=[T_prm])
        S.dma(S.sp, prm[:, l, 64:72], dt_bias[l].partition_broadcast(128), writes=[T_prm])
        S.dma(S.sp, prm[:, l, 72:80], a_log[l].partition_broadcast(128), writes=[T_prm])
        S.dma(S.sp, prm[:, l, 80:88], d_skip[l].partition_broadcast(128), writes=[T_prm])
        S.dma(S.sp, gkv[:, l, :], kv_norm[l].partition_broadcast(128), writes=[T_gkv])
    S.dma(S.sp, gfin, norm_final.partition_broadcast(128), writes=[T_gfin])
    for l in range(DEPTH):
        S.op(S.act, lambda l=l: A.activation(out=prm[:, l, 72:80], in_=prm[:, l, 72:80], func=AF.Exp),
             reads=[T_prm], writes=[T_prm])
        S.op(S.dve, lambda l=l: V.tensor_scalar(out=prm[:, l, 72:80], in0=prm[:, l, 72:80], scalar1=-1.0,
                                                scalar2=None, op0=ALU.mult), reads=[T_prm], writes=[T_prm])
    for l in range(DEPTH):
        wv = wuvp[:, l].rearrange("p (i q) c -> p i q c", q=2)
        src = w_uv[l].rearrange("l (i q) d -> l i q d", q=2)
        S.dma(S.pool, wv[:, :, 0, 0:64], src[:, :, 0, :], writes=[T_wuvp])
        S.dma(S.pool, wv[:, :, 1, 64:128], src[:, :, 1, :], writes=[T_wuvp])
    for l in range(DEPTH):
        S.dma(S.sp, sttmp.rearrange("p a b -> p (a b)"), w_uk[l].rearrange("l h d -> l (h d)"), writes=[T_sttmp])
        for hp in range(2):
            bk, T_bk = gbank()
            for hq in range(4):
                h = hp * 4 + hq
                S.op(S.pe, lambda h=h, hq=hq, bk=bk: P.transpose(
                    bk[:64, hq * 128:(hq + 1) * 128], sttmp.rearrange("p a b -> p (a b)")[:, h * 64:(h + 1) * 64], identf),
                    reads=[T_sttmp, T_cst], writes=[T_bk], sig=(hq == 3))
            S.op(S.act, lambda l=l, hp=hp, bk=bk: A.copy(
                wukT[:, l, hp * 4:(hp + 1) * 4, :], bk[:64, :].rearrange("p (a b) -> p a b", a=4)),
                reads=[T_bk], writes=[T_wukT])

    def chunk(l, c):
        return scr[l, c]

    def cast_weights(l):
        wi = w_in[l].rearrange("(k p) n -> p k n", p=128)

        def c8(c, ncol):
            return chunk(l, c).rearrange("p (k n) -> p k n", k=8)[:, :, 0:ncol]
        S.dma(S.pool, c8(C_CQ, 256), wi[:, :, S_DT:S_Q], writes=[T_scr[l][C_CQ]])
        S.dma(S.pool, c8(C_X0, 512), wi[:, :, 512:1024], writes=[T_scr[l][C_X0]])
        S.dma(S.pool, chunk(l, C_DTKV).rearrange("p (k n) -> p k n", k=8)[:, :, 0:8], wi[:, :, S_XBC:S_DT],
              writes=[T_scr[l][C_DTKV]])
        S.dma(S.pool, chunk(l, C_DTKV).rearrange("p (k n) -> p k n", k=8)[:, :, 8:168], wi[:, :, S_Q:IN_PROJ],
              writes=[T_scr[l][C_DTKV]])
        S.dma(S.pool, c8(C_X1, 512), wi[:, :, 1024:1536], writes=[T_scr[l][C_X1]])
        S.dma(S.pool, c8(C_Z, 512), wi[:, :, 0:512], writes=[T_scr[l][C_Z]])
        uq = w_uq[l].rearrange("(k p) (h c) -> p k h c", p=128, c=96)
        cu = chunk(l, C_UQ)[:, 0:2048].rearrange("p (k n) -> p k n", k=2)
        for k in range(2):
            S.dma(S.pool, cu[:, k, 0:512].rearrange("p (h d) -> p h d", d=64), uq[:, k, :, 0:64],
                  writes=[T_scr[l][C_UQ]])
            S.dma(S.pool, cu[:, k, 512:768].rearrange("p (h r) -> p h r", r=32), uq[:, k, :, 64:96],
                  writes=[T_scr[l][C_UQ]])
            rb = cu[:, k, 768:1024].rearrange("p (h r) -> p h r", r=32)
            S.dma(S.pool, rb[:, :, 0:16], uq[:, k, :, 80:96], writes=[T_scr[l][C_UQ]])
            S.dma(S.pool, rb[:, :, 16:32], uq[:, k, :, 64:80], writes=[T_scr[l][C_UQ]])
        wo = w_out[l].rearrange("(k p) n -> p k n", p=128)
        S.dma(S.pool, c8(C_O0, 512), wo[:, :, 0:512], writes=[T_scr[l][C_O0]])
        S.dma(S.pool, c8(C_O1, 512), wo[:, :, 512:1024], writes=[T_scr[l][C_O1]])
        wg = w_gate[l].rearrange("(k p) n -> p k n", p=128)
        wu = w_up[l].rearrange("(k p) n -> p k n", p=128)
        for i in range(11):
            cg = chunk(l, C_GU + i).rearrange("p (j g k n) -> p j g k n", j=2, g=2, k=8)
            for j in range(2):
                fb = 2 * i + j
                S.dma(S.pool, cg[:, j, 0], wg[:, :, fb * 128:(fb + 1) * 128], writes=[T_scr[l][C_GU + i]])
                S.dma(S.pool, cg[:, j, 1], wu[:, :, fb * 128:(fb + 1) * 128], writes=[T_scr[l][C_GU + i]])
        wd = w_down[l].rearrange("(k p) n -> p k n", p=128)
        for hf in range(2):
            for c in range(3):
                nk = 8 if c < 2 else NFB - 16
                S.dma(S.pool, chunk(l, C_D + hf * 3 + c).rearrange("p (k n) -> p k n", k=8)[:, 0:nk, :],
                      wd[:, c * 8:c * 8 + nk, hf * 512:(hf + 1) * 512], writes=[T_scr[l][C_D + hf * 3 + c]])

    cast_weights(0)
    def prefetch_sample_conv_state():
        for l in range(DEPTH):
            for sq in range(SB):
                for k in range(3):
                    S.dma(S.sp, convhist_s[l][:, :, sq, k], state_conv[l, sq, k].rearrange("(b p) -> p b", p=128),
                          writes=[T_convhist_s[l]])

    tiles = []
    for b in range(PB):
        for t in range(NTP):
            tiles.append(TileD(cfg, "p", b, t))
    if SB > 0:
        tiles.append(TileD(cfg, "s"))
    order = []
    for _ in tiles:
        for l in range(DEPTH):
            for c in [C_CQ, C_X0, C_DTKV, C_X1, C_Z, C_UQ, C_O0, C_O1] + list(range(C_GU, NCHUNK)):
                order.append((l, c))
    ws = {"issued": 0, "used": 0}

    def ws_issue():
        i = ws["issued"]
        if i >= len(order):
            return
        l, c = order[i]
        slot = i % NSLOT
        dst, src = wring[slot], scr[l, c]
        if c == C_DTKV:
            dst, src = [a.rearrange("p (k n) -> p k n", k=8)[:, :, 0:168] for a in (dst, src)]
        elif c == C_CQ:
            dst, src = [a.rearrange("p (k n) -> p k n", k=8)[:, :, 0:256] for a in (dst, src)]
        elif c == C_UQ:
            dst, src = dst[:, 0:2048], src[:, 0:2048]
        elif c in (C_D + 2, C_D + 5):
            dst, src = [a.rearrange("p (k n) -> p k n", k=8)[:, 0:NFB - 16, :] for a in (dst, src)]
        S.dma(S.sp, dst, src, reads=[T_scr[l][c]], writes=[T_wring[slot]])
        ws["issued"] = i + 1

    def ws_get(l, c, live_prev=0):
        i = ws["used"]
        assert order[i] == (l, c), (order[i], l, c)
        while ws["issued"] < min(i + NSLOT - live_prev, len(order)):
            ws_issue()
        ws["used"] = i + 1
        return wring[i % NSLOT], T_wring[i % NSLOT]

    flip = {"ev": 0}

    def evac_eng():
        flip["ev"] ^= 1
        return flip["ev"]

    def norm_stats(td, s):
        PS = td.PS
        st, T_st = nst[s], T_nst[s]
        S.op(S.act, lambda: A.activation(out=junk[:PS, :], in_=xt[:PS, s, :], func=AF.Square, accum_out=st[:PS, 0:1]),
             reads=[T_xts[s]], writes=[T_junk, T_st])
        S.op(S.act, lambda: A.activation(out=st[:PS, 1:2], in_=st[:PS, 0:1], func=AF.Ln, scale=1.0 / D_MODEL, bias=epsb[:PS, 0:1]),
             reads=[T_st, T_eps], writes=[T_st])
        S.op(S.act, lambda: A.activation(out=st[:PS, 2:3], in_=st[:PS, 1:2], func=AF.Exp, scale=-0.5), reads=[T_st], writes=[T_st])
        S.op(S.dve, lambda: V.tensor_scalar(out=xn[:PS, s, :], in0=xt[:PS, s, :], scalar1=st[:PS, 2:3], scalar2=None, op0=ALU.mult),
             reads=[T_xts[s], T_st], writes=[T_PHB])

    def norm_tr(td, s, gains):
        PS = td.PS
        bk, T_bk = gbank()
        reg = bk.bitcast(BF16)
        for kt in range(8):
            S.op(S.pe, lambda kt=kt: P.transpose(reg[:, kt * PS:(kt + 1) * PS], xn[:PS, s, kt * 128:(kt + 1) * 128], identb[:PS, :PS]),
                 reads=[T_PHB, T_cstb], writes=[T_bk], sig=(kt == 7))
        S.op(S.dve, lambda: V.tensor_tensor(out=hT[:, :, s * PS:(s + 1) * PS], in0=reg[:, 0:8 * PS].rearrange("p (k c) -> p k c", k=8),
                                            in1=gains.unsqueeze(2).broadcast_to([128, 8, PS]), op=ALU.mult),
             reads=[T_bk, T_prm], writes=[T_hT])

    def rms_to_hT(td, gains):
        for s in range(td.S):
            norm_stats(td, s)
        for s in range(td.S):
            norm_tr(td, s, gains)

    def in_proj_cq(l, td):
        PS, Sn, NT, NSEG, LSEG = td.PS, td.S, td.NT, td.NSEG, td.LSEG
        wq, T_wq = ws_get(l, C_CQ)
        wq = wq.rearrange("p (k n) -> p k n", k=8)
        cqb = []
        for m in range(2):
            bk, T_bk = gbank(hold=True)
            cqb.append((bk, T_bk))
            for kt in range(8):
                S.op(S.pe, lambda kt=kt, bk=bk, m=m: P.matmul(bk[:, :NT], wq[:, kt, m * 128:(m + 1) * 128], hT[:, kt, :NT],
                                                              start=(kt == 0), stop=(kt == 7)),
                     reads=[T_hT, T_wq], writes=[T_bk], sig=(kt == 7))
            S.op(S.act, lambda bk=bk, m=m: A.activation(out=cq2[:, m, :NT], in_=bk[:, :NT], func=AF.Square),
                 reads=[T_bk], writes=[T_PHB])
        bs, T_bs = gbank()
        for m in range(2):
            S.op(S.pe, lambda m=m, bs=bs: P.matmul(bs[:, :NT], onesb, cq2[:, m, :NT], start=(m == 0), stop=(m == 1)),
                 reads=[T_PHB, T_cstb], writes=[T_bs], sig=(m == 1))
        S.op(S.act, lambda bs=bs: A.activation(out=sqtmp[:, :NT], in_=bs[:, :NT], func=AF.Ln, scale=1.0 / Q_LORA,
                                               bias=epsb[:, 0:1]), reads=[T_bs, T_eps], writes=[T_PHB])
        S.op(S.act, lambda: A.activation(out=rstdb[:, :NT], in_=sqtmp[:, :NT], func=AF.Exp, scale=-0.5), reads=[T_PHB], writes=[T_PHB])
        for m in range(2):
            bk, T_bk = cqb[m]
            unhold(T_bk)
            S.op(S.dve, lambda bk=bk, m=m: V.scalar_tensor_tensor(out=cqn[:, m, :NT], in0=bk[:, :NT], scalar=g_q(l)[:, m:m + 1],
                                                                  in1=rstdb[:, :NT], op0=ALU.mult, op1=ALU.mult),
                 reads=[T_bk, T_PHB, T_prm], writes=[T_PHB])


    def proj_z(l, td, s, wz, T_wz):
        PS = td.PS
        bk, T_bk = gbank()
        for kt in range(8):
            S.op(S.pe, lambda kt=kt: P.matmul(bk[:PS, :], hT[:, kt, s * PS:(s + 1) * PS], wz[:, kt, :], start=(kt == 0), stop=(kt == 7)),
                 reads=[T_hT, T_wz], writes=[T_bk], sig=(kt == 7))
        S.op(S.act, lambda: A.activation(out=zs[:PS, s, :], in_=bk[:PS, :], func=AF.Silu), reads=[T_bk], writes=[T_zs])

    def proj_dtkv(l, td, s, wk, T_wk):
        PS = td.PS
        bk, T_bk = gbank()
        for kt in range(8):
            S.op(S.pe, lambda kt=kt: P.matmul(bk[:PS, 0:168], hT[:, kt, s * PS:(s + 1) * PS], wk[:, kt, 0:168],
                                              start=(kt == 0), stop=(kt == 7)),
                 reads=[T_hT, T_wk], writes=[T_bk], sig=(kt == 7))
        S.op(S.dve, lambda: V.tensor_copy(tk[:PS, s, :], bk[:PS, 0:168]), reads=[T_bk], writes=[T_tk])

    def xbc_block(l, td, blk, wx, T_wx, cs):
        PS, Sn, NT, NSEG, LSEG = td.PS, td.S, td.NT, td.NSEG, td.LSEG
        bk, T_bk = gbank()
        wc = (blk % 4) * 128
        for kt in range(8):
            S.op(S.pe, lambda kt=kt: P.matmul(bk[:, :NT], wx[:, kt, wc:wc + 128], hT[:, kt, :NT], start=(kt == 0), stop=(kt == 7)),
                 reads=[T_hT, T_wx], writes=[T_bk], sig=(kt == 7))
        u, T_u = utmp[blk % NCB], T_utmp[blk % NCB]
        uv = u[:, 0:NSEG * (LSEG + 3)].rearrange("p (g c) -> p g c", g=NSEG)
        ac, T_ac = cacc[blk % NCB], T_cacc[blk % NCB]
        acv = ac[:, 0:NT].rearrange("p (g c) -> p g c", g=NSEG)
        chT, T_ch = (convhist[l], T_convhist[l]) if td.kind == "p" else (convhist_s[l], T_convhist_s[l])
        ch = chT[:, blk, 0:NSEG, :]
        S.op(S.pool, lambda: G.tensor_copy(uv[:, :, 0:3], ch), reads=[T_ch], writes=[T_u])
        S.op(S.act, lambda: A.copy(uv[:, :, 3:3 + LSEG], bk[:, :NT].rearrange("p (g c) -> p g c", g=NSEG)), reads=[T_bk], writes=[T_u])
        S.op(S.pool, lambda: G.tensor_copy(ch, uv[:, :, LSEG:LSEG + 3]), reads=[T_u], writes=[T_ch])
        cwl = cw(l)
        S.op(S.pool, lambda: G.tensor_scalar(out=acv, in0=uv[:, :, 0:LSEG], scalar1=cwl[:, blk, 0:1], scalar2=cb(l)[:, blk:blk + 1],
                                             op0=ALU.mult, op1=ALU.add), reads=[T_u, T_prm], writes=[T_ac])
        for k in range(1, 4):
            S.op(S.dve, lambda k=k: V.scalar_tensor_tensor(out=acv, in0=uv[:, :, k:k + LSEG], scalar=cwl[:, blk, k:k + 1], in1=acv,
                                                           op0=ALU.mult, op1=ALU.add), reads=[T_u, T_prm, T_ac], writes=[T_ac])
        first = cs["first"]
        cs["first"] = False

        def silu_out():
            wl = [T_convout, T_hmid] if first else [T_convout]
            S.op(S.act, lambda: A.activation(out=convout[:, blk, :NT], in_=ac[:, :NT], func=AF.Silu), reads=[T_ac], writes=wl)
        if cs["pend"] is not None:
            cs["pend"]()
        cs["pend"] = silu_out

    def small_token(l, td, tbl_row0):
        PS, Sn = td.PS, td.S
        x0 = dtmp[:PS, 0, 0:Sn * 8].rearrange("p (s h) -> p s h", s=Sn)
        x1 = dtmp[:PS, 1, 0:Sn * 8].rearrange("p (s h) -> p s h", s=Sn)
        x2 = dtmp[:PS, 2, 0:Sn * 8].rearrange("p (s h) -> p s h", s=Sn)
        S.op(S.dve, lambda: V.tensor_tensor(out=x0, in0=tk[:PS, 0:Sn, 0:8], in1=dtb(l)[:PS].unsqueeze(1).broadcast_to([PS, Sn, 8]),
                                            op=ALU.add), reads=[T_tk, T_prm], writes=[T_dtmp])
        S.op(S.act, lambda: A.activation(out=x1, in_=x0, func=AF.Abs), reads=[T_dtmp], writes=[T_dtmp])
        S.op(S.act, lambda: A.activation(out=x1, in_=x1, func=AF.Exp, scale=-1.0), reads=[T_dtmp], writes=[T_dtmp])
        S.op(S.act, lambda: A.activation(out=x2, in_=x1, func=AF.Ln, bias=epsb[:PS, 1:2]), reads=[T_dtmp, T_eps], writes=[T_dtmp])
        S.op(S.dve, lambda: V.scalar_tensor_tensor(out=dtt[:PS, 0:Sn, :], in0=x0, scalar=0.0, in1=x2, op0=ALU.max, op1=ALU.add),
             reads=[T_dtmp], writes=[T_dtt])
        S.op(S.dve, lambda: V.tensor_tensor(out=att[:PS, 0:Sn, :], in0=dtt[:PS, 0:Sn, :],
                                            in1=A_b(l)[:PS].unsqueeze(1).broadcast_to([PS, Sn, 8]), op=ALU.mult),
             reads=[T_dtt, T_prm], writes=[T_att])
        for s in range(Sn):
            S.op(S.act, lambda s=s: A.activation(out=junk[:PS, 0:128], in_=tk[:PS, s, 8:136], func=AF.Square,
                                                 accum_out=nstat[:PS, 12 + s:13 + s]),
                 reads=[T_tk], writes=[T_junk, T_nstat])
        S.op(S.act, lambda: A.activation(out=nstat[:PS, 12:12 + Sn], in_=nstat[:PS, 12:12 + Sn], func=AF.Ln,
                                         scale=1.0 / KV_LORA, bias=epsb[:PS, 0:1]), reads=[T_nstat, T_eps], writes=[T_nstat])
        S.op(S.act, lambda: A.activation(out=nstat[:PS, 12:12 + Sn], in_=nstat[:PS, 12:12 + Sn], func=AF.Exp, scale=-0.5),
             reads=[T_nstat], writes=[T_nstat])
        for s in range(Sn):
            S.op(S.dve, lambda s=s: V.scalar_tensor_tensor(out=ckvn[:PS, s, :], in0=tk[:PS, s, 8:136],
                                                           scalar=nstat[:PS, 12 + s:13 + s], in1=gkv[:PS, l, :],
                                                           op0=ALU.mult, op1=ALU.mult),
                 reads=[T_tk, T_nstat, T_gkv], writes=[T_ckvn])
        S.op(S.act, lambda: A.copy(ckvb[:PS, 0:Sn, :], ckvn[:PS, 0:Sn, :]), reads=[T_ckvn], writes=[T_ckvb])
        cos = ropetm[:PS, 0:Sn, 0:16]
        sin = ropetm[:PS, 0:Sn, 16:32]
        r1 = tk[:PS, 0:Sn, 136:152]
        r2 = tk[:PS, 0:Sn, 152:168]
        t0 = dtmp[:PS, 3, 0:Sn * 16].rearrange("p (s h) -> p s h", s=Sn)
        t1 = dtmp[:PS, 4, 0:Sn * 16].rearrange("p (s h) -> p s h", s=Sn)
        S.op(S.dve, lambda: V.tensor_tensor(out=t0, in0=r1, in1=cos, op=ALU.mult), reads=[T_tk, T_ropetm], writes=[T_dtmp])
        S.op(S.dve, lambda: V.tensor_tensor(out=t1, in0=r2, in1=sin, op=ALU.mult), reads=[T_tk, T_ropetm], writes=[T_dtmp])
        S.op(S.dve, lambda: V.tensor_tensor(out=krn[:PS, 0:Sn, 0:16], in0=t0, in1=t1, op=ALU.subtract),
             reads=[T_dtmp], writes=[T_krn])
        S.op(S.dve, lambda: V.tensor_tensor(out=t0, in0=r1, in1=sin, op=ALU.mult), reads=[T_tk, T_ropetm], writes=[T_dtmp])
        S.op(S.dve, lambda: V.tensor_tensor(out=t1, in0=r2, in1=cos, op=ALU.mult), reads=[T_tk, T_ropetm], writes=[T_dtmp])
        S.op(S.dve, lambda: V.tensor_tensor(out=krn[:PS, 0:Sn, 16:32], in0=t0, in1=t1, op=ALU.add),
             reads=[T_dtmp], writes=[T_krn])
        for c in range(4):
            S.op(S.act, lambda c=c: A.copy(kr4b[:PS, 0:Sn, c * 32:(c + 1) * 32], krn[:PS, 0:Sn, :]), reads=[T_krn], writes=[T_kr4b])
        if td.kind == "p":
            r0 = td.t * 512
            S.dma(S.pool, o_ckv_p[l, td.b, r0:r0 + 512, :].rearrange("(s p) c -> p s c", p=128), ckvn[:, 0:4, :], reads=[T_ckvn])
            S.dma(S.pool, o_kr_p[l, td.b, r0:r0 + 512, :].rearrange("(s p) c -> p s c", p=128), krn[:, 0:4, :], reads=[T_krn])
        else:
            S.dma(S.pool, o_ckv_s[l].rearrange("s p c -> p s c"), ckvn[:PS, 0:Sn, :], reads=[T_ckvn])
            S.dma(S.pool, o_kr_s[l].rearrange("s p c -> p s c"), krn[:PS, 0:Sn, :], reads=[T_krn])

    def kv_new(l, td, s, ktile, PSk):
        col = ktile * 128
        bk, T_bk = gbank()
        reg = bk.bitcast(BF16)
        S.op(S.pe, lambda: P.transpose(reg[:, 0:PSk], ckvb[:PSk, s, :], identb[:PSk, :PSk]),
             reads=[T_ckvb, T_cstb], writes=[T_bk], sig=False)
        S.op(S.pe, lambda: P.transpose(reg[:, 128:128 + PSk], kr4b[:PSk, s, :], identb[:PSk, :PSk]),
             reads=[T_kr4b, T_cstb], writes=[T_bk])
        S.op(S.act, lambda: A.copy(ckvT[l][:, col:col + PSk], reg[:, 0:PSk]), reads=[T_bk], writes=[T_cache[l]])
        S.op(S.dve, lambda: V.tensor_copy(kr4T[l][:, col:col + PSk], reg[:, 128:128 + PSk]), reads=[T_bk], writes=[T_cache[l]])
        S.op(S.pool, lambda: G.tensor_copy(ckvtok[l][:PSk, ktile, :], ckvb[:PSk, s, :]), reads=[T_ckvb], writes=[T_cache[l]])

    def kv_past(l, sq):
        S.dma(S.pool, pastb, cache_ckv[l, sq].rearrange("(k p) c -> p k c", p=128), writes=[T_pastb])
        S.dma(S.pool, pastkr, cache_kr[l, sq].rearrange("(k p) c -> p k c", p=128), writes=[T_pastkr])
        for c in range(4):
            S.op(S.dve, lambda c=c: V.tensor_copy(pastkb[:, :, c * 32:(c + 1) * 32], pastkr), reads=[T_pastkr], writes=[T_pastkb])
        S.op(S.pool, lambda: G.tensor_copy(ckvtok[l][:, 0:NPK, :], pastb), reads=[T_pastb], writes=[T_cache[l]])
        for k in range(NPK):
            bk, T_bk = gbank()
            reg = bk.bitcast(BF16)
            S.op(S.pe, lambda k=k, reg=reg: P.transpose(reg[:, 0:128], pastb[:, k, :], identb), reads=[T_pastb, T_cstb],
                 writes=[T_bk], sig=False)
            S.op(S.pe, lambda k=k, reg=reg: P.transpose(reg[:, 128:256], pastkb[:, k, :], identb), reads=[T_pastkb, T_cstb],
                 writes=[T_bk])
            S.op(S.act, lambda k=k, reg=reg: A.copy(ckvT[l][:, k * 128:(k + 1) * 128], reg[:, 0:128]), reads=[T_bk],
                 writes=[T_cache[l]])
            S.op(S.dve, lambda k=k, reg=reg: V.tensor_copy(kr4T[l][:, k * 128:(k + 1) * 128], reg[:, 128:256]), reads=[T_bk],
                 writes=[T_cache[l]])

    def ssd_sub(l, td, s):
        L = td.PS
        c0 = s * L
        XB, T_XB = XBt[s % 2], T_XBt[s % 2]
        ex, T_ex = ex2[s % 2], T_ex2[s % 2]
        a_s = att[:L, s, :]
        bk, T_bk = gbank()
        reg = bk.bitcast(BF16)
        for q in range(6):
            S.op(S.pe, lambda q=q: P.transpose(reg[:L, q * 128:(q + 1) * 128], convout[:, q, c0:c0 + L], identb),
                 reads=[T_convout, T_cstb], writes=[T_bk], sig=(q == 5))
        S.op(S.act, lambda: A.copy(XB[:L, :], reg[:L, 0:768]), reads=[T_bk], writes=[T_XB])
        S.op(S.pool, lambda: G.tensor_tensor(out=Rt[:L, :, :L], in0=Uf[:L, :L].unsqueeze(1).broadcast_to([L, 8, L]),
                                             in1=a_s.unsqueeze(2).broadcast_to([L, 8, L]), op=ALU.mult),
             reads=[T_cst, T_att], writes=[T_Rt])
        bm, T_bm = gbank()
        S.op(S.pe, lambda: P.matmul(bm[:L, 0:8], Uf[:L, :L], a_s, start=True, stop=True, skip_group_check=True),
             reads=[T_cst, T_att], writes=[T_bm], sig=False)
        S.op(S.pe, lambda: P.matmul(bm[:L, 8:16], Mstf[:L, :L], a_s, start=True, stop=True, skip_group_check=True),
             reads=[T_cst, T_att], writes=[T_bm], sig=False)
        S.op(S.pe, lambda: P.matmul(bm[:, 16:24], onesf[:L, :], a_s, start=True, stop=True, skip_group_check=True),
             reads=[T_cst, T_att], writes=[T_bm])
        S.op(S.act, lambda: A.activation(out=ex[:L, 0:16], in_=bm[:L, 0:16], func=AF.Exp), reads=[T_bm], writes=[T_ex])
        S.op(S.act, lambda: A.activation(out=ex[:, 16:24], in_=bm[:, 16:24], func=AF.Exp), reads=[T_bm], writes=[T_ex])
        bc, T_bc = gbank()
        for g in range(2):
            S.op(S.pe, lambda g=g: P.matmul(bc[:L, g * L:(g + 1) * L], convout[:, 4 + g, c0:c0 + L],
                                            convout[:, 6 + g, c0:c0 + L], start=True, stop=True, skip_group_check=True),
                 reads=[T_convout], writes=[T_bc], sig=(g == 1))
        yield
        S.op(S.dve, lambda: V.tensor_tensor(out=ex[:L, 24:32], in0=ex[:L, 8:16], in1=dtt[:L, s, :], op=ALU.mult),
             reads=[T_ex, T_dtt], writes=[T_ex])
        S.op(S.dve, lambda: V.tensor_tensor(out=CBm[:L, :, :L], in0=bc[:L, 0:2 * L].rearrange("p (g c) -> p g c", g=2),
                                            in1=Uf[:L, :L].unsqueeze(1).broadcast_to([L, 2, L]), op=ALU.mult),
             reads=[T_bc, T_cst], writes=[T_CBm])
        segb = []
        for hf in range(2):
            bs, T_bs = gbank()
            segb.append((bs, T_bs))
            S.op(S.pe, lambda hf=hf, bs=bs: P.matmul(bs[:L, 0:4 * L].rearrange("p (h c) -> p h c", h=4), Mstf[:L, :L],
                                                     Rt[:L, hf * 4:(hf + 1) * 4, :L], start=True, stop=True),
                 reads=[T_cst, T_Rt], writes=[T_bs])
        yield
        for hf in range(2):
            bs, T_bs = segb[hf]
            S.op(S.act, lambda hf=hf, bs=bs: A.activation(out=Et[:L, hf * 4:(hf + 1) * 4, :L],
                                                          in_=bs[:L, 0:4 * L].rearrange("p (h c) -> p h c", h=4), func=AF.Exp),
                 reads=[T_bs], writes=[T_Et])
        S.op(S.pool, lambda: G.tensor_tensor(out=xD[:L, :].rearrange("p (h d) -> p h d", h=8),
                                             in0=XB[:L, 0:512].rearrange("p (h d) -> p h d", h=8),
                                             in1=D_b(l)[:L].unsqueeze(2).broadcast_to([L, 8, 64]), op=ALU.mult),
             reads=[T_XB, T_prm], writes=[T_xD])
        S.op(S.pool, lambda: G.tensor_tensor(out=xw[:L, :].rearrange("p (h d) -> p h d", h=8),
                                             in0=XB[:L, 0:512].rearrange("p (h d) -> p h d", h=8),
                                             in1=ex[:L, 24:32].unsqueeze(2).broadcast_to([L, 8, 64]), op=ALU.mult),
             reads=[T_XB, T_ex], writes=[T_xw])
        yield
        for g in range(2):
            S.op(S.dve, lambda g=g: V.tensor_tensor(out=Et[:L, g * 4:(g + 1) * 4, :L], in0=Et[:L, g * 4:(g + 1) * 4, :L],
                                                    in1=CBm[:L, g, :L].unsqueeze(1).broadcast_to([L, 4, L]), op=ALU.mult),
                 reads=[T_Et, T_CBm], writes=[T_Et])
        S.op(S.dve, lambda: V.tensor_tensor(out=Wb[:L, :, :L], in0=Et[:L, :, :L],
                                            in1=dtt[:L, s, :].unsqueeze(2).broadcast_to([L, 8, L]), op=ALU.mult),
             reads=[T_Et, T_dtt], writes=[T_Wb])
        yield
        by, T_by = gbank()
        for h in range(8):
            S.op(S.pe, lambda h=h: P.matmul(by[:L, h * 64:(h + 1) * 64], Wb[:L, h, :L], XB[:L, h * 64:(h + 1) * 64],
                                            start=(h == 0), stop=False, skip_group_check=True),
                 reads=[T_Wb, T_XB], writes=[T_by], sig=False)
        S.op(S.pe, lambda: P.matmul(by[:L, :], identb[:L, :L], xD[:L, :], start=False, stop=True, skip_group_check=True),
             reads=[T_xD, T_cstb], writes=[T_by])
        bi, T_bi = gbank()
        for g in range(2):
            S.op(S.pe, lambda g=g: P.matmul(bi[:L, g * 256:(g + 1) * 256], convout[:, 6 + g, c0:c0 + L],
                                            hTb[l][:, g * 256:(g + 1) * 256], start=(g == 0), stop=(g == 1),
                                            skip_group_check=True),
                 reads=[T_convout, T_hT_st[l]], writes=[T_bi], sig=(g == 1))
        yield
        S.op(S.dve, lambda: V.tensor_tensor(out=ytmp[:L, :].rearrange("p (h d) -> p h d", h=8),
                                            in0=bi[:L, :].rearrange("p (h d) -> p h d", h=8),
                                            in1=ex[:L, 0:8].unsqueeze(2).broadcast_to([L, 8, 64]), op=ALU.mult),
             reads=[T_bi, T_ex], writes=[T_ytmp])
        S.op(S.dve, lambda: V.tensor_tensor(out=ytmp[:L, :], in0=ytmp[:L, :], in1=by[:L, :], op=ALU.add),
             reads=[T_by, T_ytmp], writes=[T_ytmp])
        S.op(S.dve, lambda: V.tensor_tensor(out=yg[:L, :], in0=ytmp[:L, :], in1=zs[:L, s, :], op=ALU.mult),
             reads=[T_ytmp, T_zs], writes=[T_yg])
        bh, T_bh = gbank()
        for g in range(2):
            S.op(S.pe, lambda g=g: P.matmul(bh[:, g * 256:(g + 1) * 256], XB[:L, 512 + g * 128:512 + (g + 1) * 128],
                                            xw[:L, g * 256:(g + 1) * 256], start=(g == 0), stop=(g == 1),
                                            skip_group_check=True),
                 reads=[T_XB, T_xw], writes=[T_bh], sig=(g == 1))
        yield
        S.op(S.dve, lambda: V.tensor_tensor(out=hTf[l].rearrange("p (h d) -> p h d", h=8),
                                            in0=hTf[l].rearrange("p (h d) -> p h d", h=8),
                                            in1=ex[:, 16:24].unsqueeze(2).broadcast_to([128, 8, 64]), op=ALU.mult),
             reads=[T_ex, T_hT_st[l]], writes=[T_hT_st[l]])
        S.op(S.dve, lambda: V.tensor_tensor(out=hTf[l], in0=hTf[l], in1=bh, op=ALU.add),
             reads=[T_bh, T_hT_st[l]], writes=[T_hT_st[l]])
        S.op(S.act, lambda: A.activation(out=junk[:L, 0:512], in_=yg[:L, :], func=AF.Square, accum_out=nstat[:L, 0:1]),
             reads=[T_yg], writes=[T_junk, T_nstat])
        S.op(S.act, lambda: A.activation(out=nstat[:L, 1:2], in_=nstat[:L, 0:1], func=AF.Ln, scale=1.0 / 512,
                                         bias=epsb[:L, 0:1]), reads=[T_nstat, T_eps], writes=[T_nstat])
        yield
        S.op(S.act, lambda: A.copy(hTb[l], hTf[l]), reads=[T_hT_st[l]], writes=[T_hT_st[l]])
        S.op(S.act, lambda: A.activation(out=nstat[:L, 2:3], in_=nstat[:L, 1:2], func=AF.Exp, scale=-0.5), reads=[T_nstat], writes=[T_nstat])
        S.op(S.act, lambda: A.activation(out=mixtok[:L, s, :], in_=yg[:L, :], func=AF.Copy, scale=nstat[:L, 2:3]),
             reads=[T_yg, T_nstat], writes=[T_mixtok])
        yield

    def state_zero(l):
        S.op(S.pool, lambda: G.memset(hTf[l], 0.0), writes=[T_hT_st[l]])
        S.op(S.pool, lambda: G.memset(hTb[l], 0.0), writes=[T_hT_st[l]])

    def state_load(l, sq):
        S.dma(S.pool, sttmp, state_ssm[l, sq].rearrange("h p n -> (h p) n").rearrange("(q r) n -> r q n", r=128), writes=[T_sttmp])
        bk, T_bk = gbank()
        for q in range(4):
            S.op(S.pe, lambda q=q, bk=bk: P.transpose(bk[:, q * 128:(q + 1) * 128], sttmp[:, q, :], identf),
                 reads=[T_sttmp, T_cst], writes=[T_bk], sig=(q == 3))
        S.op(S.dve, lambda bk=bk: V.tensor_copy(hTf[l], bk), reads=[T_bk], writes=[T_hT_st[l]])
        S.op(S.act, lambda bk=bk: A.copy(hTb[l], bk), reads=[T_bk], writes=[T_hT_st[l]])

    def state_store(l, dst):
        bk, T_bk = gbank()
        for q in range(4):
            S.op(S.pe, lambda q=q, bk=bk: P.transpose(bk[:, q * 128:(q + 1) * 128], hTf[l][:, q * 128:(q + 1) * 128], identf),
                 reads=[T_hT_st[l], T_cst], writes=[T_bk], sig=(q == 3))
        S.op(S.dve, lambda bk=bk: V.tensor_copy(sttmp, bk.rearrange("p (q n) -> p q n", q=4)), reads=[T_bk], writes=[T_sttmp])
        S.dma(S.pool, dst.rearrange("h p n -> (h p) n").rearrange("(q r) n -> r q n", r=128), sttmp, reads=[T_sttmp])

    def q_rope(l, td, wq, T_wq):
        NT = td.NT
        for g in range(2):
            ba, T_ba = gbank()
            bb, T_bb = gbank()
            for kt in range(2):
                S.op(S.pe, lambda kt=kt, g=g, ba=ba: P.matmul(ba[:, :NT], wq[:, kt, 512 + g * 128:512 + (g + 1) * 128],
                                                              cqn[:, kt, :NT], start=(kt == 0), stop=(kt == 1)),
                     reads=[T_PHB, T_wq], writes=[T_ba], sig=(kt == 1))
            for kt in range(2):
                S.op(S.pe, lambda kt=kt, g=g, bb=bb: P.matmul(bb[:, :NT], wq[:, kt, 768 + g * 128:768 + (g + 1) * 128],
                                                              cqn[:, kt, :NT], start=(kt == 0), stop=(kt == 1)),
                     reads=[T_PHB, T_wq], writes=[T_bb], sig=(kt == 1))
            S.op(S.dve, lambda ba=ba: V.tensor_tensor(out=ropet[0][:, :NT], in0=ba[:, :NT], in1=ropefm[:, 0, :NT], op=ALU.mult),
                 reads=[T_ba, T_ropefm], writes=[T_ropet[0]])
            S.op(S.dve, lambda bb=bb: V.tensor_tensor(out=ropet[1][:, :NT], in0=bb[:, :NT], in1=ropefm[:, 1, :NT], op=ALU.mult),
                 reads=[T_bb, T_ropefm], writes=[T_ropet[1]])
            S.op(S.pool, lambda g=g: G.tensor_tensor(out=qr4[g][:, :NT], in0=ropet[0][:, :NT], in1=ropet[1][:, :NT], op=ALU.add),
                 reads=[T_ropet[0], T_ropet[1]], writes=[T_qr4[g]])

    def q_head(l, td, h, wq, T_wq):
        NT = td.NT
        i = h % 2
        bn, T_bn = gbank()
        for kt in range(2):
            S.op(S.pe, lambda kt=kt, bn=bn: P.matmul(bn[:64, :NT], wq[:, kt, h * 64:(h + 1) * 64], cqn[:, kt, :NT],
                                                     start=(kt == 0), stop=(kt == 1)),
                 reads=[T_PHB, T_wq], writes=[T_bn], sig=(kt == 1))
        S.op(S.act, lambda bn=bn: A.copy(qn_h[i][:, :NT], bn[:64, :NT]), reads=[T_bn], writes=[T_qn[i]])
        bl, T_bl = gbank()
        S.op(S.pe, lambda bl=bl: P.matmul(bl[:, :NT], wukT[:, l, h, :], qn_h[i][:, :NT], start=True, stop=True),
             reads=[T_qn[i], T_wukT], writes=[T_bl])
        S.op(S.dve, lambda bl=bl: V.tensor_copy(qlat_h[i][:, :NT], bl[:, :NT]), reads=[T_bl], writes=[T_qlat[i]])
        g, hh = h // 4, h % 4
        S.op(S.act, lambda: A.activation(out=qrz[i][:, :NT], in_=qr4[g][:, :NT], func=AF.Copy, scale=hmask[:, hh:hh + 1]),
             reads=[T_qr4[g], T_cst], writes=[T_qrz[i]])

    def attn_head(l, td, h, q0, nq, keytiles, fin_box):
        i = h % 2
        nk = len(keytiles)

        def scores(ki):
            kt, nkeys, c0, dj = keytiles[ki]
            bs, T_bs = abank_s()
            a0, a1 = q0 + c0, q0 + nq
            col = kt * 128
            S.op(S.pe, lambda: P.matmul(bs[:nkeys, a0:a1], ckvT[l][:, col:col + nkeys], qlat_h[i][:, a0:a1], start=True, stop=False),
                 reads=[T_cache[l], T_qlat[i]], writes=[T_bs], sig=False)
            S.op(S.pe, lambda: P.matmul(bs[:nkeys, a0:a1], kr4T[l][:, col:col + nkeys], qrz[i][:, a0:a1], start=False, stop=True),
                 reads=[T_cache[l], T_qrz[i]], writes=[T_bs])
            if dj is None:
                idx = gstate["pt"] % NPT
                gstate["pt"] += 1
                pt, T_pt = PTf[idx], T_PTf[idx]
                S.op(S.act, lambda: A.activation(out=pt[:nkeys, a0:a1], in_=bs[:nkeys, a0:a1], func=AF.Exp, scale=SCALE),
                     reads=[T_bs], writes=[T_pt])
            else:
                pt, T_pt = PTd[dj], T_PTd[dj]
                S.op(S.act, lambda: A.activation(out=pt[:, a0 + 64:a1], in_=bs[:, a0 + 64:a1], func=AF.Exp, scale=SCALE),
                     reads=[T_bs], writes=[T_pt])
                S.op(S.act, lambda: A.activation(out=pt[0:64, a0:a0 + 64], in_=bs[0:64, a0:a0 + 64], func=AF.Exp, scale=SCALE),
                     reads=[T_bs], writes=[T_pt])
            return pt, T_pt

        def pv(ki, pt, T_pt):
            kt, nkeys, c0, dj = keytiles[ki]
            a0, a1 = q0 + c0, q0 + nq
            S.op(S.pe, lambda: P.matmul(bankO[:, a0:a1], ckvtok[l][:nkeys, kt, :], pt[:nkeys, a0:a1], start=(ki == 0),
                                        stop=(ki == nk - 1), skip_group_check=True),
                 reads=[T_cache[l], T_pt], writes=[T_bankO], sig=False)
            S.op(S.pe, lambda: P.matmul(bankZ[:, a0:a1], onesb[:nkeys, :], pt[:nkeys, a0:a1], start=(ki == 0),
                                        stop=(ki == nk - 1), skip_group_check=True),
                 reads=[T_cstb, T_pt], writes=[T_bankZ], sig=(ki == nk - 1))

        pend = None
        prev_fin = fin_box["fin"]
        for ki in range(nk):
            cur = scores(ki)
            if ki == 1 and prev_fin is not None:
                prev_fin()
                prev_fin = None
            if pend is not None:
                pv(ki - 1, *pend)
            pend = cur
            yield
        if prev_fin is not None:
            prev_fin()
        pv(nk - 1, *pend)
        b0, b1 = q0, q0 + nq
        S.op(S.act, lambda: A.activation(out=rz[:, b0:b1], in_=bankZ[:, b0:b1], func=AF.Ln), reads=[T_bankZ], writes=[T_rz])
        S.op(S.act, lambda: A.activation(out=rz[:, b0:b1], in_=rz[:, b0:b1], func=AF.Exp, scale=-1.0), reads=[T_rz], writes=[T_rz])
        ol, T_ol = olat2[i], T_olat2[i]
        S.op(S.dve, lambda: V.tensor_tensor(out=ol[:, b0:b1], in0=bankO[:, b0:b1], in1=rz[:, b0:b1], op=ALU.mult),
             reads=[T_bankO, T_rz], writes=[T_ol])

        def fin():
            bg, T_bg = gbank()
            S.op(S.pe, lambda: P.matmul(bg[:, b0:b1], wuvp[:, l, h, :], ol[:, b0:b1], start=True, stop=True),
                 reads=[T_wuvp, T_ol], writes=[T_bg])
            r0 = (h % 2) * 64
            kt_o = 4 + h // 2
            if evac_eng():
                S.op(S.act, lambda: A.copy(mixT[r0:r0 + 64, kt_o, b0:b1], bg[r0:r0 + 64, b0:b1]), reads=[T_bg], writes=[T_mixT])
            else:
                S.op(S.dve, lambda: V.tensor_copy(mixT[r0:r0 + 64, kt_o, b0:b1], bg[r0:r0 + 64, b0:b1]), reads=[T_bg], writes=[T_mixT])
        fin_box["fin"] = fin

    hview = PHA.rearrange("p (f n) -> p f n", f=NFB)
    qlat_all = hview[:, 0:8, 256:512]
    qrz_all = hview[:, 12:20, 256:512]
    T_qall = Tk("qall")

    def q_all_sample(l, td, wq, T_wq):
        NT = td.NT
        for h in range(8):
            i = h % 2
            bn, T_bn = gbank()
            for kt in range(2):
                S.op(S.pe, lambda kt=kt: P.matmul(bn[:64, :NT], wq[:, kt, h * 64:(h + 1) * 64], cqn[:, kt, :NT],
                                                  start=(kt == 0), stop=(kt == 1)),
                     reads=[T_PHB, T_wq], writes=[T_bn], sig=(kt == 1))
            S.op(S.act, lambda: A.copy(qn_h[i][:, :NT], bn[:64, :NT]), reads=[T_bn], writes=[T_qn[i]])
            bl, T_bl = gbank()
            S.op(S.pe, lambda: P.matmul(bl[:, :NT], wukT[:, l, h, :], qn_h[i][:, :NT], start=True, stop=True),
                 reads=[T_qn[i], T_wukT], writes=[T_bl])
            S.op(S.dve, lambda: V.tensor_copy(qlat_all[:, h, :NT], bl[:, :NT]), reads=[T_bl], writes=[T_qall])
            g, hh = h // 4, h % 4
            S.op(S.act, lambda: A.activation(out=qrz_all[:, h, :NT], in_=qr4[g][:, :NT], func=AF.Copy, scale=hmask[:, hh:hh + 1]),
                 reads=[T_qr4[g], T_cst], writes=[T_qall])

    def attn_sample_seg(l, td, s, keytiles):
        DSq = td.PS
        q0 = s * DSq
        nk = len(keytiles)
        NQ = 8 * DSq

        def scores(ki):
            kt, nkeys, c0, dj = keytiles[ki]
            bs, T_bs = abank_s()
            col = kt * 128
            bsv = bs[:nkeys, 0:NQ].rearrange("p (h c) -> p h c", h=8)
            S.op(S.pe, lambda: P.matmul(bsv, ckvT[l][:, col:col + nkeys], qlat_all[:, :, q0:q0 + DSq], start=True, stop=False),
                 reads=[T_cache[l], T_qall], writes=[T_bs], sig=False)
            S.op(S.pe, lambda: P.matmul(bsv, kr4T[l][:, col:col + nkeys], qrz_all[:, :, q0:q0 + DSq], start=False, stop=True),
                 reads=[T_cache[l], T_qall], writes=[T_bs])
            idx = gstate["pt"] % NPT
            gstate["pt"] += 1
            pt, T_pt = PTf[idx], T_PTf[idx]
            S.op(S.act, lambda: A.activation(out=pt[:nkeys, 0:NQ], in_=bs[:nkeys, 0:NQ], func=AF.Exp, scale=SCALE),
                 reads=[T_bs], writes=[T_pt])
            return pt, T_pt

        def pv(ki, pt, T_pt):
            kt, nkeys, c0, dj = keytiles[ki]
            S.op(S.pe, lambda: P.matmul(bankO[:, 0:NQ], ckvtok[l][:nkeys, kt, :], pt[:nkeys, 0:NQ], start=(ki == 0),
                                        stop=(ki == nk - 1), skip_group_check=True),
                 reads=[T_cache[l], T_pt], writes=[T_bankO], sig=False)
            S.op(S.pe, lambda: P.matmul(bankZ[:, 0:NQ], onesb[:nkeys, :], pt[:nkeys, 0:NQ], start=(ki == 0),
                                        stop=(ki == nk - 1), skip_group_check=True),
                 reads=[T_cstb, T_pt], writes=[T_bankZ], sig=(ki == nk - 1))

        pend = None
        for ki in range(nk):
            cur = scores(ki)
            if pend is not None:
                pv(ki - 1, *pend)
            pend = cur
            yield
        pv(nk - 1, *pend)
        S.op(S.act, lambda: A.activation(out=rz[:, 0:NQ], in_=bankZ[:, 0:NQ], func=AF.Ln), reads=[T_bankZ], writes=[T_rz])
        S.op(S.act, lambda: A.activation(out=rz[:, 0:NQ], in_=rz[:, 0:NQ], func=AF.Exp, scale=-1.0), reads=[T_rz], writes=[T_rz])
        ol, T_ol = olat2[s % 2], T_olat2[s % 2]
        S.op(S.dve, lambda: V.tensor_tensor(out=ol[:, 0:NQ], in0=bankO[:, 0:NQ], in1=rz[:, 0:NQ], op=ALU.mult),
             reads=[T_bankO, T_rz], writes=[T_ol])
        for h in range(8):
            bg, T_bg = gbank()
            S.op(S.pe, lambda h=h, bg=bg: P.matmul(bg[:, 0:DSq], wuvp[:, l, h, :], ol[:, h * DSq:(h + 1) * DSq], start=True, stop=True),
                 reads=[T_wuvp, T_ol], writes=[T_bg])
            r0 = (h % 2) * 64
            kt_o = 4 + h // 2
            if evac_eng():
                S.op(S.act, lambda bg=bg: A.copy(mixT[r0:r0 + 64, kt_o, q0:q0 + DSq], bg[r0:r0 + 64, 0:DSq]), reads=[T_bg], writes=[T_mixT])
            else:
                S.op(S.dve, lambda bg=bg: V.tensor_copy(mixT[r0:r0 + 64, kt_o, q0:q0 + DSq], bg[r0:r0 + 64, 0:DSq]), reads=[T_bg], writes=[T_mixT])

    def mix_out(l, td):
        PS, Sn, NT = td.PS, td.S, td.NT
        for kt in range(4):
            bk, T_bk = gbank()
            reg = bk.bitcast(BF16)
            for s in range(Sn):
                S.op(S.pe, lambda s=s, kt=kt, reg=reg: P.transpose(reg[:, s * PS:(s + 1) * PS],
                                                                   mixtok[:PS, s, kt * 128:(kt + 1) * 128], identb[:PS, :PS]),
                     reads=[T_mixtok, T_cstb], writes=[T_bk], sig=(s == Sn - 1))
            S.op(S.dve, lambda kt=kt, reg=reg: V.tensor_scalar(out=mixT[:, kt, :NT], in0=reg[:, :NT], scalar1=g_ssd(l)[:, kt:kt + 1],
                                                               scalar2=None, op0=ALU.mult),
                 reads=[T_bk, T_prm], writes=[T_mixT])
        for hf in range(2):
            wo, T_wo = ws_get(l, C_O0 + hf)
            wo = wo.rearrange("p (k n) -> p k n", k=8)
            for s in range(Sn):
                bk, T_bk = gbank()
                for kt in range(8):
                    S.op(S.pe, lambda kt=kt, s=s, bk=bk, wo=wo: P.matmul(bk[:PS, :], mixT[:, kt, s * PS:(s + 1) * PS], wo[:, kt, :],
                                                                          start=(kt == 0), stop=(kt == 7)),
                         reads=[T_mixT, T_wo], writes=[T_bk], sig=(kt == 7))
                S.op(S.dve, lambda s=s, bk=bk, hf=hf: V.tensor_tensor(out=xt[:PS, s, hf * 512:(hf + 1) * 512],
                                                                      in0=xt[:PS, s, hf * 512:(hf + 1) * 512], in1=bk[:PS, :], op=ALU.add),
                     reads=[T_bk, T_xts[s]], writes=[T_xts[s]])

    def ffn(l, td, on_ready, on_done):
        PS, Sn, NT = td.PS, td.S, td.NT
        first = True
        for i in range(11):
            wg_, T_wg = ws_get(l, C_GU + i)
            wg_ = wg_.rearrange("p (j g k n) -> p j g k n", j=2, g=2, k=8)
            for j in range(2):
                fb = 2 * i + j
                bg, T_bg = gbank()
                bu, T_bu = gbank()
                for kt in range(8):
                    S.op(S.pe, lambda kt=kt, bg=bg, j=j, wg_=wg_: P.matmul(bg[:, :NT], wg_[:, j, 0, kt, :], hT[:, kt, :NT],
                                                                            start=(kt == 0), stop=(kt == 7)),
                         reads=[T_hT, T_wg], writes=[T_bg], sig=(kt == 7))
                for kt in range(8):
                    S.op(S.pe, lambda kt=kt, bu=bu, j=j, wg_=wg_: P.matmul(bu[:, :NT], wg_[:, j, 1, kt, :], hT[:, kt, :NT],
                                                                            start=(kt == 0), stop=(kt == 7)),
                         reads=[T_hT, T_wg], writes=[T_bu], sig=(kt == 7))
                sgi, T_sgi = sg[fb % 2], T_sg[fb % 2]
                S.op(S.act, lambda bg=bg, sgi=sgi: A.activation(out=sgi[:, :NT], in_=bg[:, :NT], func=AF.Silu),
                     reads=[T_bg], writes=[T_sgi])
                wl = [T_hmid]
                if first:
                    wl = [T_hmid, T_convout, T_mixtok, T_mixT]
                    first = False
                S.op(S.dve, lambda bu=bu, sgi=sgi, fb=fb: V.tensor_tensor(out=hmid[:, fb, :NT], in0=sgi[:, :NT], in1=bu[:, :NT],
                                                                          op=ALU.mult),
                     reads=[T_bu, T_sgi], writes=wl)
        accs = [gbank() for s in range(Sn)]
        for c in range(3):
            wd_, T_wd = ws_get(l, C_D + c)
            wd_ = wd_.rearrange("p (k n) -> p k n", k=8)
            nk = 8 if c < 2 else NFB - 16
            for k in range(nk):
                fb = c * 8 + k
                for s in range(Sn):
                    bk, T_bk = accs[s]
                    S.op(S.pe, lambda k=k, s=s, bk=bk, fb=fb, wd_=wd_: P.matmul(bk[:PS, :], hmid[:, fb, s * PS:(s + 1) * PS],
                                                                                 wd_[:, k, :], start=(fb == 0), stop=(fb == NFB - 1)),
                         reads=[T_hmid, T_wd], writes=[T_bk], sig=(fb == NFB - 1))
        for s in range(Sn):
            bk, T_bk = accs[s]
            S.op(S.dve, lambda s=s, bk=bk: V.tensor_tensor(out=xt[:PS, s, 0:512], in0=xt[:PS, s, 0:512], in1=bk[:PS, :], op=ALU.add),
                 reads=[T_bk, T_xts[s]], writes=[T_xts[s]])
        wds = []
        for c in range(3):
            wd_, T_wd = ws_get(l, C_D + 3 + c, live_prev=c)
            wds.append((wd_.rearrange("p (k n) -> p k n", k=8), T_wd))
        for s in range(Sn):
            bk, T_bk = banks[4 + s], T_bank[4 + s]
            for fb in range(NFB):
                wd_, T_wd = wds[fb // 8]
                S.op(S.pe, lambda s=s, bk=bk, fb=fb, wd_=wd_: P.matmul(bk[:PS, :], hmid[:, fb, s * PS:(s + 1) * PS],
                                                                        wd_[:, fb % 8, :], start=(fb == 0), stop=(fb == NFB - 1)),
                     reads=[T_hmid, T_wd], writes=[T_bk], sig=(fb == NFB - 1))
            S.op(S.dve, lambda s=s, bk=bk: V.tensor_tensor(out=xt[:PS, s, 512:1024], in0=xt[:PS, s, 512:1024], in1=bk[:PS, :], op=ALU.add),
                 reads=[T_bk, T_xts[s]], writes=[T_xts[s]])
            on_ready(s)
        on_done()

    def final_norm_store_sub(td, s):
        PS = td.PS
        st, T_st = nst[s], T_nst[s]
        S.op(S.act, lambda: A.activation(out=junk[:PS, :], in_=xt[:PS, s, :], func=AF.Square, accum_out=st[:PS, 0:1]),
             reads=[T_xts[s]], writes=[T_junk, T_st])
        S.op(S.act, lambda: A.activation(out=st[:PS, 1:2], in_=st[:PS, 0:1], func=AF.Ln, scale=1.0 / D_MODEL, bias=epsb[:PS, 0:1]),
             reads=[T_st, T_eps], writes=[T_st])
        S.op(S.act, lambda: A.activation(out=st[:PS, 2:3], in_=st[:PS, 1:2], func=AF.Exp, scale=-0.5), reads=[T_st], writes=[T_st])
        S.op(S.dve, lambda: V.scalar_tensor_tensor(out=xt[:PS, s, :], in0=xt[:PS, s, :], scalar=st[:PS, 2:3], in1=gfin[:PS, :],
                                                   op0=ALU.mult, op1=ALU.mult),
             reads=[T_xts[s], T_st, T_gfin], writes=[T_xts[s]])
        if td.kind == "p":
            r0 = td.t * 512 + s * 128
            S.dma(S.pool, y_prompt[td.b, r0:r0 + 128, :], xt[:, s, :], reads=[T_xts[s]])
        else:
            S.dma(S.pool, y_sample[s], xt[:PS, s, :], reads=[T_xts[s]])

    def load_x_sub(td, s):
        if td.kind == "p":
            r0 = td.t * 512 + s * 128
            S.dma(S.sp, xt[:, s, :], x_prompt[td.b, r0:r0 + 128, :], writes=[T_xts[s]])
        else:
            S.dma(S.sp, xt[:td.PS, s, :], x_sample[s], writes=[T_xts[s]])

    hT_ready = False
    for ti, td in enumerate(tiles):
        PS, Sn, NT = td.PS, td.S, td.NT
        if ti == 0 and PB == 0:
            for s in range(Sn):
                load_x_sub(td, s)
        if td.kind == "p":
            r0 = td.t * 512
            S.dma(S.sp, ropefm, rope_fm_d[:, :, r0:r0 + 512], writes=[T_ropefm])
            S.dma(S.sp, ropetm[:, 0:4, :], rope_tm_d[r0:r0 + 512, :].rearrange("(s p) c -> p s c", p=128), writes=[T_ropetm])
        else:
            for s in range(Sn):
                S.dma(S.sp, ropefm[:, :, s * DS:(s + 1) * DS], rope_fm_d[:, :, SEQ:SEQ + DS], writes=[T_ropefm])
                S.dma(S.sp, ropetm[:PS, s, :], rope_tm_d[SEQ:SEQ + DS, :], writes=[T_ropetm])
        for l in range(DEPTH):
            if td.kind == "p":
                if td.t == 0:
                    state_zero(l)
                    S.op(S.pool, lambda l=l: G.memset(convhist[l], 0.0), writes=[T_convhist[l]])
            if not hT_ready:
                rms_to_hT(td, g_mix(l))
            hT_ready = False
            in_proj_cq(l, td)
            cs = {"first": True, "pend": None}
            wx, T_wx = ws_get(l, C_X0)
            wx = wx.rearrange("p (k n) -> p k n", k=8)
            wk, T_wk = ws_get(l, C_DTKV, live_prev=1)
            wk = wk.rearrange("p (k n) -> p k n", k=8)
            for i in range(4):
                xbc_block(l, td, i, wx, T_wx, cs)
                if i < Sn:
                    proj_dtkv(l, td, i, wk, T_wk)
            small_token(l, td, 0)
            wx, T_wx = ws_get(l, C_X1)
            wx = wx.rearrange("p (k n) -> p k n", k=8)
            wz, T_wz = ws_get(l, C_Z, live_prev=1)
            wz = wz.rearrange("p (k n) -> p k n", k=8)
            for i in range(4):
                xbc_block(l, td, 4 + i, wx, T_wx, cs)
                if i < Sn:
                    proj_z(l, td, i, wz, T_wz)
            cs["pend"]()
            if td.kind == "p":
                for s in range(Sn):
                    kv_new(l, td, s, td.t * 4 + s, 128)
            wq, T_wq = ws_get(l, C_UQ)
            wq = wq[:, 0:2048].rearrange("p (k n) -> p k n", k=2)
            q_rope(l, td, wq, T_wq)
            if td.kind == "p":
                kts = []
                for kt in range(td.t * 4 + 4):
                    j = kt - td.t * 4
                    if j < 0:
                        kts.append((kt, 128, 0, None))
                    else:
                        kts.append((kt, 128, 128 * j, j))

                def attn_all(l=l, td=td, kts=kts, wq=wq, T_wq=T_wq):
                    fin_box = {"fin": None}
                    q_head(l, td, 0, wq, T_wq)
                    for h in range(8):
                        if h + 1 < 8:
                            q_head(l, td, h + 1, wq, T_wq)
                        yield from attn_head(l, td, h, 0, 512, kts, fin_box)
                    fin_box["fin"]()

                def ssd_all(l=l, td=td):
                    gens = [ssd_sub(l, td, s) for s in range(td.S)]
                    OFF = 8
                    for k in range(OFF * (td.S - 1) + 8):
                        for s in reversed(range(td.S)):
                            if 0 <= k - OFF * s < 8:
                                next(gens[s])
                        yield

                ga, gs = attn_all(), ssd_all()
                ratio = max(1, (8 * len(kts)) // (8 * Sn))
                done_a = done_s = False
                while not (done_a and done_s):
                    if not done_s:
                        try:
                            next(gs)
                        except StopIteration:
                            done_s = True
                    for _ in range(ratio):
                        if not done_a:
                            try:
                                next(ga)
                            except StopIteration:
                                done_a = True
                if td.t == NTP - 1:
                    state_store(l, o_ssm_p[l, td.b])
                    for k in range(3):
                        S.dma(S.pool, o_conv_p[l, td.b, k].rearrange("(b p) -> p b", p=128), convhist[l][:, :, 0, k],
                              reads=[T_convhist[l]])
            else:
                def ssd_sample(l=l, td=td):
                    for s in range(td.S):
                        state_load(l, s)
                        yield
                        yield from ssd_sub(l, td, s)
                        state_store(l, o_ssm_s[l, s])
                        yield
                def attn_sample(l=l, td=td, wq=wq, T_wq=T_wq):
                    q_all_sample(l, td, wq, T_wq)
                    yield
                    for s in range(td.S):
                        kv_past(l, s)
                        kv_new(l, td, s, NPK, DS)
                        kts = [(kt, 128, 0, None) for kt in range(NPK)] + [(NPK, DS, 0, None)]
                        yield from attn_sample_seg(l, td, s, kts)
                        yield

                ga, gs = attn_sample(), ssd_sample()
                done_a = done_s = False
                while not (done_a and done_s):
                    if not done_s:
                        try:
                            next(gs)
                        except StopIteration:
                            done_s = True
                    if not done_a:
                        try:
                            next(ga)
                        except StopIteration:
                            done_a = True
                for s in range(Sn):
                    for k in range(3):
                        S.dma(S.pool, o_conv_s[l, s, k].rearrange("(b p) -> p b", p=128), convhist_s[l][:, :, s, k],
                              reads=[T_convhist_s[l]])
            if td is tiles[0] and l == 0:
                for l2 in range(1, DEPTH):
                    cast_weights(l2)
                prefetch_sample_conv_state()
            mix_out(l, td)
            rms_to_hT(td, g_ffn(l))
            if l + 1 < DEPTH:
                def on_ready(s, td=td, l=l):
                    norm_stats(td, s)
                    if s >= 1:
                        norm_tr(td, s - 1, g_mix(l + 1))

                def on_done(td=td, l=l):
                    norm_tr(td, td.S - 1, g_mix(l + 1))
                ffn(l, td, on_ready, on_done)
                hT_ready = True
            else:
                nxt = tiles[ti + 1] if ti + 1 < len(tiles) else None

                def on_ready(s, td=td, nxt=nxt):
                    final_norm_store_sub(td, s)
                    if nxt is not None and s < nxt.S:
                        load_x_sub(nxt, s)
                ffn(l, td, on_ready, lambda: None)

    S.finish()
    nci.__exit__(None, None, None)
    stats = {e.name: (e.nins, e.nwait) for e in (S.pe, S.act, S.dve, S.pool, S.sp)}
    return nc, stats


def host_consts(cfg):
    c = np.zeros((128, 5 * 128), np.float32)
    j = np.arange(128)
    c[:, 0:128] = np.eye(128, dtype=np.float32)
    c[:, 128:256] = (j[:, None] <= j[None, :]).astype(np.float32)
    c[:, 256:384] = (j[:, None] > j[None, :]).astype(np.float32)
    c[:, 384:512] = 1.0
    for hh in range(4):
        c[hh * 32:(hh + 1) * 32, 512 + hh] = 1.0
    pos = np.concatenate([np.arange(cfg.SEQ), cfg.PAST + np.arange(cfg.DS)]).astype(np.float32)
    inv = (1.0 / (ROPE_BASE ** (np.arange(0, QK_ROPE, 2, dtype=np.float32) / QK_ROPE))).astype(np.float32)
    ang = pos[:, None] * inv[None, :]
    cos, sin = np.cos(ang).astype(np.float32), np.sin(ang).astype(np.float32)
    tm = np.concatenate([cos, sin], axis=1).astype(np.float32)
    fm = np.zeros((128, 2, pos.shape[0]), np.float32)
    for hh in range(4):
        fm[hh * 32:hh * 32 + 16, 0] = cos.T
        fm[hh * 32 + 16:hh * 32 + 32, 0] = cos.T
        fm[hh * 32:hh * 32 + 16, 1] = -sin.T
        fm[hh * 32 + 16:hh * 32 + 32, 1] = sin.T
    return c, fm, tm


_CACHE = {}


def run(cfg, inputs, ncores):
    key = (cfg.PB, cfg.SEQ, cfg.SB, cfg.PAST, cfg.DEPTH)
    if key not in _CACHE:
        _CACHE[key] = build(cfg)
    nc, stats = _CACHE[key]
    c, fm, tm = host_consts(cfg)
    PB, SB = cfg.PB, cfg.SB
    f = lambda a: np.ascontiguousarray(np.asarray(a, dtype=np.float32))
    in_maps = []
    for i in range(ncores):
        m = {}
        m["x_prompt"] = f(inputs["x_prompt"][i * PB:(i + 1) * PB])
        m["x_sample"] = f(inputs["x_sample"][i * SB:(i + 1) * SB])
        m["cache_mla_ckv"] = f(inputs["cache_mla_ckv"][:, i * SB:(i + 1) * SB])
        m["cache_mla_krope"] = f(inputs["cache_mla_krope"][:, i * SB:(i + 1) * SB])
        m["state_ssm"] = f(inputs["state_ssm"][:, i * SB:(i + 1) * SB])
        m["state_conv"] = f(inputs["state_conv"][:, i * SB:(i + 1) * SB])
        for k in ("w_in", "w_uq", "w_uk", "w_uv", "w_out", "norm_mix", "q_norm", "kv_norm", "ssd_norm", "conv_w", "conv_b",
                  "dt_bias", "a_log", "d_skip", "norm_ffn", "w_gate", "w_up", "w_down", "norm_final"):
            m[k] = f(inputs[k])
        m["consts"] = c
        m["rope_fm"] = fm
        m["rope_tm"] = tm
        in_maps.append(m)
    res = run_bass_kernel_spmd(nc, in_maps, core_ids=list(range(ncores)))
    R = res.results
    cat0 = lambda k: np.concatenate([np.asarray(r[k], dtype=np.float32) for r in R], axis=0)
    cat1 = lambda k: np.concatenate([np.asarray(r[k], dtype=np.float32) for r in R], axis=1)
    return (cat0("y_prompt"), cat0("y_sample"), cat1("o_ckv_p"), cat1("o_kr_p"), cat1("o_ssm_p"), cat1("o_conv_p"),
            cat1("o_ckv_s"), cat1("o_kr_s"), cat1("o_ssm_s"), cat1("o_conv_s"))


def kernel(**inputs):
    cfg = Cfg(PB=4, SEQ=2048, SB=4, PAST=1024, DEPTH=2)
    return run(cfg, inputs, 8)
```
